# Optimizing a Trainium2 kernel written in Bass

```python
import math
import jax
import jax.numpy as jnp
from jax import lax
import numpy as np

D_MODEL = 1024
BATCH = 32
SEQ = 256
DEPTH = 4
DEC_BATCH = 8
DEC_SEQ = 4096
PAST_LEN = 512

GRID_W = 64
EPS = 1e-6
N_HEADS = 8
N_KV_HEADS = 2
Q_PER_KV = N_HEADS // N_KV_HEADS
HEAD_DIM = 128
ATTN_Q = N_HEADS * HEAD_DIM
ATTN_KV = N_KV_HEADS * HEAD_DIM
ROPE_THETA = 10000.0
Q_BLOCK = 128
CONV_WIDTH = 512
CONV_K = 3
DN_HEADS = 4
DN_DK = 128
DN_DV = 128
DN_QK = DN_HEADS * DN_DK
DN_VW = DN_HEADS * DN_DV
DN_CONV_K = 3
DN_CHUNK = 64
D_FF = (((8 * D_MODEL + 2) // 3 + 255) // 256) * 256
IN_SIZES = (CONV_WIDTH, CONV_WIDTH, CONV_WIDTH,
            ATTN_Q, ATTN_KV, ATTN_KV,
            DN_QK, DN_QK, DN_VW, DN_VW, 2 * DN_HEADS, 2 * DN_HEADS,
            3 * D_MODEL)
D_IN = sum(IN_SIZES)
SPLIT_IDX = tuple(int(s) for s in np.cumsum(IN_SIZES)[:-1])

kernel_name = 'hybrid_diffusion_trunk_step'


def rms_norm(x, g):
    xf = x.astype(jnp.float32)
    y = xf * lax.rsqrt(jnp.mean(xf * xf, axis=-1, keepdims=True) + EPS)
    return (y * g.astype(jnp.float32)).astype(x.dtype)


def l2_normalize(x):
    return x * lax.rsqrt(jnp.sum(x * x, axis=-1, keepdims=True) + EPS)


def depthwise_conv(x, w):
    pad = w.shape[0] // 2
    return lax.conv_general_dilated(
        x, w[:, None, :].astype(x.dtype), window_strides=(1,), padding=((pad, pad),),
        dimension_numbers=('NWC', 'WIO', 'NWC'), feature_group_count=x.shape[-1])


def axial_rope_angles(rows):
    row_id = jnp.repeat(jnp.arange(rows, dtype=jnp.float32), GRID_W)
    col_id = jnp.tile(jnp.arange(GRID_W, dtype=jnp.float32), rows)
    n_freq = HEAD_DIM // 4
    inv_freq = ROPE_THETA ** (-jnp.arange(n_freq, dtype=jnp.float32) / n_freq)
    ang = jnp.concatenate([row_id[:, None] * inv_freq, col_id[:, None] * inv_freq], axis=-1)
    return jnp.cos(ang), jnp.sin(ang)


def apply_rope(x, cos, sin):
    xf = x.astype(jnp.float32).reshape(*x.shape[:-1], HEAD_DIM // 2, 2)
    x1, x2 = xf[..., 0], xf[..., 1]
    c, s = cos[:, None, :], sin[:, None, :]
    out = jnp.stack([x1 * c - x2 * s, x1 * s + x2 * c], axis=-1).reshape(x.shape)
    return out.astype(x.dtype)


def block_attention(q, k, v):
    b, tq = q.shape[:2]
    nb = tq // Q_BLOCK
    qb = jnp.moveaxis(q.reshape(b, nb, Q_BLOCK, N_KV_HEADS, Q_PER_KV, HEAD_DIM), 1, 0)
    scale = HEAD_DIM ** -0.5

    def one_block(qi):
        s = jnp.einsum('bqhgd,bshd->bhgqs', qi, k, preferred_element_type=jnp.float32) * scale
        p = jax.nn.softmax(s, axis=-1)
        return jnp.einsum('bhgqs,bshd->bqhgd', p.astype(v.dtype), v)

    o = lax.map(one_block, qb)
    return jnp.moveaxis(o, 0, 1).reshape(b, tq, ATTN_Q)


def chunked_gated_delta(q, k, v, g, beta, s0):
    b, t, h, _ = k.shape
    n = t // DN_CHUNK

    def chunks(a):
        a = a.reshape(b, n, DN_CHUNK, h, *a.shape[3:])
        return jnp.moveaxis(jnp.moveaxis(a, 1, 0), 3, 2)

    qc, kc, vc, gc, bc = chunks(q), chunks(k), chunks(v), chunks(g), chunks(beta)
    G = jnp.cumsum(gc, axis=-1)
    idx = jnp.arange(DN_CHUNK)
    lower_incl = idx[:, None] >= idx[None, :]
    strict = (idx[:, None] > idx[None, :]).astype(jnp.float32)
    decay = jnp.exp(jnp.where(lower_incl, G[..., :, None] - G[..., None, :], -jnp.inf))
    kb = kc * bc[..., None]
    lmat = jnp.einsum('nbhik,nbhjk->nbhij', kb, kc) * decay * strict
    amat = lmat + jnp.eye(DN_CHUNK, dtype=jnp.float32)
    rhs = jnp.concatenate([vc * bc[..., None], kb * jnp.exp(G)[..., None]], axis=-1)
    sol = lax.linalg.triangular_solve(amat, rhs, left_side=True, lower=True, unit_diagonal=True)
    u, w = sol[..., :DN_DV], sol[..., DN_DV:]

    def step(S, inp):
        q_i, k_i, u_i, w_i, G_i, dec_i = inp
        v_new = u_i - jnp.einsum('bhck,bhkv->bhcv', w_i, S)
        attn = jnp.einsum('bhik,bhjk->bhij', q_i, k_i) * dec_i
        o = (jnp.einsum('bhck,bhkv->bhcv', q_i * jnp.exp(G_i)[..., None], S)
             + jnp.einsum('bhij,bhjv->bhiv', attn, v_new))
        g_last = G_i[..., -1:]
        S = (S * jnp.exp(g_last)[..., None]
             + jnp.einsum('bhck,bhcv->bhkv', k_i * jnp.exp(g_last - G_i)[..., None], v_new))
        return S, o

    s_fin, o = lax.scan(step, s0, (qc, kc, u, w, G, decay))
    o = jnp.moveaxis(jnp.moveaxis(o, 2, 3), 0, 1).reshape(b, t, h, DN_DV)
    return o, s_fin


def gated_deltanet_bidir(q_raw, k_raw, v_raw, z, beta_logit, a_logit, conv_w, a_log, dt_bias, norm_g, s0):
    b, t, _ = q_raw.shape
    qkv = jax.nn.silu(depthwise_conv(jnp.concatenate([q_raw, k_raw, v_raw], axis=-1), conv_w))
    q, k, v = jnp.split(qkv.astype(jnp.float32), 3, axis=-1)
    q = l2_normalize(q.reshape(b, t, DN_HEADS, DN_DK)) * (DN_DK ** -0.5)
    k = l2_normalize(k.reshape(b, t, DN_HEADS, DN_DK))
    v = v.reshape(b, t, DN_HEADS, DN_DV)
    beta = jax.nn.sigmoid(beta_logit.astype(jnp.float32)).reshape(b, t, 2, DN_HEADS)
    g = -jnp.exp(a_log.astype(jnp.float32)) * jax.nn.softplus(
        a_logit.astype(jnp.float32).reshape(b, t, 2, DN_HEADS) + dt_bias.astype(jnp.float32))
    s0 = s0.astype(jnp.float32)
    o_f, s_f = chunked_gated_delta(q, k, v, g[:, :, 0], beta[:, :, 0], s0[:, 0])
    rev = lambda a: jnp.flip(a, axis=1)
    o_b, s_b = chunked_gated_delta(rev(q), rev(k), rev(v), rev(g[:, :, 1]), rev(beta[:, :, 1]), s0[:, 1])
    o = o_f + rev(o_b)
    o = rms_norm(o, norm_g) * jax.nn.silu(z.astype(jnp.float32).reshape(b, t, DN_HEADS, DN_DV))
    return o.reshape(b, t, DN_VW).astype(q_raw.dtype), jnp.stack([s_f, s_b], axis=1)


def trunk_layer(x, cvec, lp, rope=None, ctx=None):
    b, t, _ = x.shape
    mod = jax.nn.silu(cvec) @ lp['w_mod'] + lp['b_mod']
    sh1, sc1, gt1, sh2, sc2, gt2 = jnp.split(mod[:, None, :], 6, axis=-1)
    h = rms_norm(x, lp['g_pre1']) * (1 + sc1) + sh1
    (cb, cc, cx, aq, ak, av, dn_q, dn_k, dn_v, dn_z, dn_beta, dn_a, gate_logits) = jnp.split(
        h @ lp['w_in'], SPLIT_IDX, axis=-1)

    ya = (cb * depthwise_conv(cc * cx, lp['conv_w'])) @ lp['w_pa']

    q = rms_norm(aq.reshape(b, t, N_HEADS, HEAD_DIM), lp['g_qn'])
    k = rms_norm(ak.reshape(b, t, N_KV_HEADS, HEAD_DIM), lp['g_kn'])
    v = av.reshape(b, t, N_KV_HEADS, HEAD_DIM)
    if ctx is None:
        keys, vals = k, v
        s0 = jnp.zeros((b, 2, DN_HEADS, DN_DK, DN_DV), jnp.float32)
    else:
        cos, sin = rope
        q = apply_rope(q, cos, sin)
        k = apply_rope(k, cos, sin)
        keys = jnp.concatenate([ctx[0].astype(k.dtype), k], axis=1)
        vals = jnp.concatenate([ctx[1].astype(v.dtype), v], axis=1)
        s0 = ctx[2]
    yb = block_attention(q.reshape(b, t, N_KV_HEADS, Q_PER_KV, HEAD_DIM), keys, vals) @ lp['w_pb']

    yc, s_fin = gated_deltanet_bidir(dn_q, dn_k, dn_v, dn_z, dn_beta, dn_a, lp['dn_conv_w'],
                                     lp['dn_a_log'], lp['dn_dt_bias'], lp['dn_norm_g'], s0)
    yc = yc @ lp['w_pc']

    ga, gb, gc = jnp.split(jax.nn.sigmoid(gate_logits), 3, axis=-1)
    mix = (ga * ya + gb * yb + gc * yc) @ lp['w_o']
    x = x + gt1 * rms_norm(mix, lp['g_post1'])

    h2 = rms_norm(x, lp['g_pre2']) * (1 + sc2) + sh2
    ffn = (jax.nn.silu(h2 @ lp['w_gate']) * (h2 @ lp['w_up'])) @ lp['w_down']
    x = x + gt2 * rms_norm(ffn, lp['g_post2'])
    return x, (k, v, s_fin.astype(x.dtype))


def setup_inputs(seed: int = 0) -> dict:
    key = jax.random.key(seed)
    ks = jax.random.split(key, 32)
    f32 = jnp.float32
    D = D_MODEL

    def nrm(k, shape, scale):
        return jax.random.normal(k, shape, f32) * scale

    def gain(k, shape):
        return 1.0 + 0.05 * jax.random.normal(k, shape, f32)

    dt = jnp.exp(jax.random.uniform(ks[20], (DEPTH, 2, DN_HEADS), f32, math.log(1e-3), math.log(1e-1)))
    return {
        'x_prompt': nrm(ks[0], (BATCH, SEQ, D), 1.0),
        'x_sample': nrm(ks[1], (DEC_BATCH, DEC_SEQ, D), 1.0),
        'cache_k': nrm(ks[2], (DEC_BATCH, DEPTH, PAST_LEN, N_KV_HEADS, HEAD_DIM), 1.0),
        'cache_v': nrm(ks[3], (DEC_BATCH, DEPTH, PAST_LEN, N_KV_HEADS, HEAD_DIM), 1.0),
        'state_dn': nrm(ks[4], (DEC_BATCH, DEPTH, 2, DN_HEADS, DN_DK, DN_DV), 0.1),
        'c': nrm(ks[5], (DEC_BATCH, D), 1.0),
        'c_ctx': nrm(ks[6], (D,), 1.0),
        'w_mod': nrm(ks[7], (DEPTH, D, 6 * D), D ** -0.5),
        'b_mod': nrm(ks[8], (DEPTH, 6 * D), 0.02),
        'g_pre1': gain(ks[9], (DEPTH, D)),
        'g_post1': gain(ks[10], (DEPTH, D)),
        'g_pre2': gain(ks[11], (DEPTH, D)),
        'g_post2': gain(ks[12], (DEPTH, D)),
        'w_in': nrm(ks[13], (DEPTH, D, D_IN), D ** -0.5),
        'conv_w': nrm(ks[14], (DEPTH, CONV_K, CONV_WIDTH), CONV_K ** -0.5),
        'g_qn': gain(ks[15], (DEPTH, HEAD_DIM)),
        'g_kn': gain(ks[16], (DEPTH, HEAD_DIM)),
        'dn_conv_w': nrm(ks[17], (DEPTH, DN_CONV_K, 2 * DN_QK + DN_VW), DN_CONV_K ** -0.5),
        'dn_a_log': jnp.log(jax.random.uniform(ks[18], (DEPTH, 2, DN_HEADS), f32, 1.0, 16.0)),
        'dn_dt_bias': dt + jnp.log(-jnp.expm1(-dt)),
        'dn_norm_g': gain(ks[19], (DEPTH, DN_DV)),
        'w_pa': nrm(ks[21], (DEPTH, CONV_WIDTH, D), CONV_WIDTH ** -0.5),
        'w_pb': nrm(ks[22], (DEPTH, ATTN_Q, D), ATTN_Q ** -0.5),
        'w_pc': nrm(ks[23], (DEPTH, DN_VW, D), DN_VW ** -0.5),
        'w_o': nrm(ks[24], (DEPTH, D, D), D ** -0.5),
        'w_gate': nrm(ks[25], (DEPTH, D, D_FF), D ** -0.5),
        'w_up': nrm(ks[26], (DEPTH, D, D_FF), D ** -0.5),
        'w_down': nrm(ks[27], (DEPTH, D_FF, D), D_FF ** -0.5),
    }


def reference(x_prompt, x_sample, cache_k, cache_v, state_dn, c, c_ctx, w_mod, b_mod,
              g_pre1, g_post1, g_pre2, g_post2, w_in, conv_w, g_qn, g_kn, dn_conv_w,
              dn_a_log, dn_dt_bias, dn_norm_g, w_pa, w_pb, w_pc, w_o, w_gate, w_up, w_down):
    rows = x_sample.shape[1] // GRID_W
    rope = axial_rope_angles(rows)
    y_prompt, y_sample = x_prompt, x_sample
    ks, vs, ss = [], [], []
    for l in range(DEPTH):
        lp = {'w_mod': w_mod[l], 'b_mod': b_mod[l], 'g_pre1': g_pre1[l], 'g_post1': g_post1[l],
              'g_pre2': g_pre2[l], 'g_post2': g_post2[l], 'w_in': w_in[l], 'conv_w': conv_w[l],
              'g_qn': g_qn[l], 'g_kn': g_kn[l], 'dn_conv_w': dn_conv_w[l], 'dn_a_log': dn_a_log[l],
              'dn_dt_bias': dn_dt_bias[l], 'dn_norm_g': dn_norm_g[l], 'w_pa': w_pa[l], 'w_pb': w_pb[l],
              'w_pc': w_pc[l], 'w_o': w_o[l], 'w_gate': w_gate[l], 'w_up': w_up[l], 'w_down': w_down[l]}
        y_prompt, (k_l, v_l, s_l) = trunk_layer(y_prompt, c_ctx[None, :], lp)
        ks.append(k_l)
        vs.append(v_l)
        ss.append(s_l)
        y_sample, _ = trunk_layer(y_sample, c, lp, rope=rope,
                                  ctx=(cache_k[:, l], cache_v[:, l], state_dn[:, l]))
    new_k = jnp.stack(ks, axis=1)
    new_v = jnp.stack(vs, axis=1)
    new_state_dn = jnp.stack(ss, axis=1)
    return (y_prompt, y_sample, new_k, new_v, new_state_dn)
```

```python
import numpy as np
from contextlib import ExitStack
import concourse.bass as bass
import concourse.mybir as mybir
from concourse.bass_utils import run_bass_kernel_spmd

F32 = mybir.dt.float32
BF16 = mybir.dt.bfloat16
AF = mybir.ActivationFunctionType
ALU = mybir.AluOpType
AX = mybir.AxisListType

D = 1024
DEPTH = 4
TS = 4096
TP = 1024
NT = TS + TP
TB = 512
NB = NT // TB
D_IN = 8208
D_FF = 2816
EPS = 1e-6
PAST = 512


class Buf:
    __slots__ = ("name", "w", "r", "multi", "excl")

    def __init__(self, name, multi=False, excl=False):
        self.name = name
        self.w = {}
        self.r = {}
        self.multi = multi
        self.excl = excl


class Sched:
    def __init__(self, nc, es):
        self.nc = nc
        self.eng = {"pe": nc.tensor, "act": nc.scalar, "dve": nc.vector,
                    "pool": nc.gpsimd, "sp": nc.sync}
        self.csem = {e: es.enter_context(nc.semaphore("c_" + e)) for e in ("pe", "act", "dve", "pool")}
        self.cnt = {e: 0 for e in self.csem}
        self.known = {e: {} for e in self.eng}
        self.dsem = {}
        self.dval = {}
        self.dnext = {}
        for q, n in (("sp", 24), ("pool", 16), ("act", 4)):
            self.dsem[q] = [es.enter_context(nc.semaphore(f"d_{q}{i}")) for i in range(n)]
            self.dval[q] = [0] * n
            self.dnext[q] = 0
        self.nwait = 0
        self.nins = 0

    def _need(self, e, sem, val, out):
        if val <= 0:
            return
        k = self.known[e]
        if k.get(sem, 0) >= val:
            return
        if e == "pe" and sem is self.csem["pe"]:
            return
        k[sem] = val
        for i, (s2, v2) in enumerate(out):
            if s2 is sem:
                out[i] = (sem, max(val, v2))
                return
        out.append((sem, val))

    def _wait(self, e, sem, val):
        out = []
        self._need(e, sem, val, out)
        for s2, v2 in out:
            self.eng[e].wait_ge(s2, v2)
            self.nwait += 1

    def _deps(self, e, reads, writes):
        out = []
        own = self.csem.get(e)
        for b in reads:
            for sem, val in b.w.items():
                self._need(e, sem, val, out)
            if b.excl:
                for sem, val in b.r.items():
                    if sem is not own:
                        self._need(e, sem, val, out)
        for b in writes:
            if not b.multi:
                for sem, val in b.w.items():
                    self._need(e, sem, val, out)
            for sem, val in b.r.items():
                self._need(e, sem, val, out)
        return out

    def _emit_waits(self, e, waits, ins_fn):
        for s2, v2 in waits[:-1]:
            self.eng[e].wait_ge(s2, v2)
            self.nwait += 1
        ins = ins_fn()
        if waits:
            ins._wait_ge(*waits[-1])
        return ins

    def _record(self, sem, val, reads, writes):
        for b in reads:
            if b.r.get(sem, 0) < val:
                b.r[sem] = val
        for b in writes:
            if b.multi:
                if b.w.get(sem, 0) < val:
                    b.w[sem] = val
            else:
                b.w = {sem: val}
            b.r = {}

    def op(self, e, fn, reads=(), writes=(), inc=True):
        waits = self._deps(e, reads, writes)
        ins = self._emit_waits(e, waits, lambda: fn(self.eng[e]))
        self.nins += 1
        sem = self.csem[e]
        if inc:
            self.cnt[e] += 1
            ins.then_inc(sem, 1)
            val = self.cnt[e]
        else:
            val = self.cnt[e] + 1
        self._record(sem, val, reads, writes)
        return ins

    def dma(self, q, out, in_, reads=(), writes=(), **kw):
        waits = self._deps(q, reads, writes)
        i = self.dnext[q]
        self.dnext[q] = (i + 1) % len(self.dsem[q])
        sem = self.dsem[q][i]
        self._need(q, sem, self.dval[q][i], waits)
        self.dval[q][i] += 16
        ins = self._emit_waits(q, waits, lambda: self.eng[q].dma_start(out=out, in_=in_, **kw))
        ins.then_inc(sem, 16)
        self.nins += 1
        self._record(sem, self.dval[q][i], reads, writes)

    def all_tokens(self):
        toks = [(self.csem[e], self.cnt[e]) for e in self.csem]
        for q in self.dsem:
            toks += list(zip(self.dsem[q], self.dval[q]))
        return toks

    def barrier(self):
        toks = self.all_tokens()
        for e in self.eng:
            for sem, val in toks:
                self._wait(e, sem, val)

    def finish(self):
        for sem, val in self.all_tokens():
            self._wait("sp", sem, val)


C_CB, C_CC, C_CX = 0, 512, 1024
C_AQ, C_AK, C_AV = 1536, 2560, 2816
C_DQ, C_DK, C_DV, C_DZ = 3072, 3584, 4096, 4608
C_BETA, C_A = 5120, 5128
C_GATE = 5136
PP_BMOD = 0
PP_GPRE1 = 48
PP_GPOST1 = 56
PP_GPRE2 = 64
PP_GPOST2 = 72
PP_CONVW = 80
PP_DNCONVW = 92
PP_GQN = 128
PP_GKN = 129
PP_DNG = 130
PP_ALOG = 131
PP_DTB = 139
PPW = 148


NCONST = 10


def build_program(depth=DEPTH, stop_after=None, debug_out=(), skip=()):
    nc = bass.Bass("TRN2", target_bir_lowering=False)
    dt = nc.dram_tensor

    def din(name, shape, dtype=F32):
        return dt(name, list(shape), dtype, kind="ExternalInput").ap()

    def dout(name, shape, dtype=F32):
        return dt(name, list(shape), dtype, kind="ExternalOutput").ap()

    def dscr(name, shape, dtype=F32):
        kind = "ExternalOutput" if name in debug_out else "Internal"
        return dt(name, list(shape), dtype, kind=kind).ap()

    xs_in = din("xs", [TS, D])
    xp_in = din("xp", [TP, D])
    cvec_in = din("cvec", [128, 8, 2])
    pp_in = din("pp", [DEPTH, 128, PPW])
    consts_in = din("consts", [128, NCONST, 128])
    rope_in = din("rope", [128, 2, TS])
    ck_in = din("ck", [DEPTH, PAST, 256])
    cvv_in = din("cvv", [DEPTH, PAST, 256])
    sdn_in = din("sdn", [DEPTH, 2, 4, 128, 128])
    w_mod = din("w_mod", [DEPTH, D, 6 * D])
    w_in = din("w_in", [DEPTH, D, D_IN])
    w_pa = din("w_pa", [DEPTH, 512, D])
    w_pb = din("w_pb", [DEPTH, 1024, D])
    w_pc = din("w_pc", [DEPTH, 512, D])
    w_o = din("w_o", [DEPTH, D, D])
    w_gate = din("w_gate", [DEPTH, D, D_FF])
    w_up = din("w_up", [DEPTH, D, D_FF])
    w_down = din("w_down", [DEPTH, D_FF, D])

    xT = dscr("xT", [D, NT])
    projT = dscr("projT", [D_IN, NT])
    brT = dscr("brT", [2048, NT], BF16)
    actT = dscr("actT", [D_FF, NT], BF16)
    y_s = dout("y_s", [TS, D])
    y_p = dout("y_p", [TP, D])
    nk_out = dout("nk", [4, DEPTH, 256, 256])
    nv_out = dout("nv", [4, DEPTH, 256, 256])
    ns_out = dout("ns", [4, DEPTH, 2, 4, 128, 128])

    xT_v = xT.rearrange("(kc p) t -> p kc t", p=128)

    with ExitStack() as es:
        S = Sched(nc, es)

        uid = {"n": 0}

        def sb(name, shape, dtype=F32, ctx=es):
            uid["n"] += 1
            return ctx.enter_context(nc.sbuf_tensor(f"sb{uid['n']}_{name}", list(shape), dtype))

        psum = [es.enter_context(nc.psum_tensor(f"ps{i}", [128, 512], F32)) for i in range(8)]
        psb = [Buf(f"ps{i}", excl=True) for i in range(8)]

        cst = sb("cst", [128, NCONST, 128])
        b_cst = Buf("cst")
        S.dma("sp", cst[:], consts_in[:, :, :], writes=[b_cst])
        ident = cst[:, 0, :]
        ones = cst[:, 1, :]
        pswapT = cst[:, 2, :]
        onesb = sb("onesb", [128, 128], BF16)
        S.op("dve", lambda e: e.tensor_copy(onesb[:], ones), reads=[b_cst], writes=[b_cst])

        b_xT = Buf("xT", multi=True)
        b_proj = Buf("projT", multi=True)
        b_br = Buf("brT", multi=True)
        b_actT = Buf("actT", multi=True)
        b_out = Buf("outs", multi=True)

        def evac(k, out, in_, reads, writes):
            if k % 2 == 0:
                S.op("act", lambda e: e.copy(out, in_), reads=reads, writes=writes)
            else:
                S.op("dve", lambda e: e.tensor_copy(out, in_), reads=reads, writes=writes)

        with ExitStack() as ph:
            xin = [sb(f"xin{i}", [128, D], ctx=ph) for i in range(2)]
            b_xin = [Buf(f"xin{i}") for i in range(2)]
            xo = [sb(f"xo{i}", [128, 8, 128], ctx=ph) for i in range(2)]
            b_xo = [Buf(f"xo{i}") for i in range(2)]
            for tt in range(NT // 128):
                i = tt % 2
                src = xs_in[tt * 128:(tt + 1) * 128, :] if tt < TS // 128 else \
                    xp_in[(tt - TS // 128) * 128:(tt - TS // 128 + 1) * 128, :]
                S.dma("sp", xin[i][:], src, writes=[b_xin[i]])
                for half in range(2):
                    pb = (tt * 2 + half) % 8
                    for j in range(4):
                        kc = half * 4 + j
                        S.op("pe", lambda e, kc=kc, j=j, pb=pb, i=i: e.transpose(
                            psum[pb][:, j * 128:(j + 1) * 128], xin[i][:, kc * 128:(kc + 1) * 128], ident),
                            reads=[b_xin[i], b_cst], writes=[psb[pb]], inc=(j == 3))
                    evac(half, xo[i][:, half * 4:(half + 1) * 4, :].rearrange("p a b -> p (a b)"), psum[pb][:],
                         [psb[pb]], [b_xo[i]])
                S.dma("pool", xT_v[:, :, tt * 128:(tt + 1) * 128], xo[i][:], reads=[b_xo[i]], writes=[b_xT])
            S.barrier()

        b_hT = Buf("hT", multi=True)
        b_hTd = Buf("hT_d", multi=True)
        pp = sb("pp", [128, PPW])
        b_pp = Buf("pp")
        cv = sb("cvec", [128, 8, 2])
        b_cv = Buf("cvec")
        scv = sb("scvec", [128, 8, 2])
        b_scv = Buf("scvec")
        modT = sb("modT", [128, 48, 2])
        b_mod = Buf("modT")
        modc = sb("modc", [128, 6, 8, 2])
        b_modc = Buf("modc")
        sq = [sb(f"sq{i}", [128, TB]) for i in range(2)]
        b_sq = [Buf(f"sq{i}") for i in range(2)]
        rstd = sb("rstd", [128, TB])
        b_rstd = Buf("rstd")
        S.dma("sp", cv[:], cvec_in[:, :, :], writes=[b_cv])
        S.op("act", lambda e: e.activation(out=scv[:], in_=cv[:], func=AF.Silu), reads=[b_cv], writes=[b_scv])

        PS_NORM = 7

        sqi = {"n": 0}

        def rms_rstd(src_fn, b_src, nch, n, inv_n):
            for kc in range(nch):
                qi = sqi["n"] % 2
                sqi["n"] += 1
                S.op("act", lambda e: e.activation(out=sq[qi][:, 0:n], in_=src_fn(kc), func=AF.Square),
                     reads=[b_src], writes=[b_sq[qi]])
                S.op("pe", lambda e: e.matmul(psum[PS_NORM][:, 0:n], ones, sq[qi][:, 0:n],
                                              start=(kc == 0), stop=(kc == nch - 1)),
                     reads=[b_sq[qi], b_cst], writes=[psb[PS_NORM]])
            S.op("dve", lambda e: e.tensor_scalar(rstd[:, 0:n], psum[PS_NORM][:, 0:n], inv_n, EPS, ALU.mult, ALU.add),
                 reads=[psb[PS_NORM]], writes=[b_rstd])
            S.op("act", lambda e: e.activation(out=rstd[:, 0:n], in_=rstd[:, 0:n], func=AF.Ln),
                 reads=[b_rstd], writes=[b_rstd])
            S.op("act", lambda e: e.activation(out=rstd[:, 0:n], in_=rstd[:, 0:n], func=AF.Exp, scale=-0.5),
                 reads=[b_rstd], writes=[b_rstd])

        tmpn = [sb(f"tmpn{i}", [128, TB]) for i in range(2)]
        b_tmpn = [Buf(f"tmpn{i}") for i in range(2)]

        def norm_mod_block(xb_t, b_x, tb, which, dst_fn, b_dst):
            g = 0 if tb < TS // TB else 1
            rms_rstd(lambda kc: xb_t[:, kc, :], b_x, 8, TB, 1.0 / D)
            for kc in range(8):
                j = kc % 2
                S.op("dve", lambda e: e.scalar_tensor_tensor(
                    tmpn[j][:], xb_t[:, kc, :], modc[:, which * 3 + 0, kc, g:g + 1], rstd[:], ALU.mult, ALU.mult),
                    reads=[b_x, b_modc, b_rstd], writes=[b_tmpn[j]])
                S.op("act", lambda e: e.activation(
                    out=dst_fn(kc), in_=tmpn[j][:], func=AF.Identity,
                    bias=modc[:, which * 3 + 1, kc, g:g + 1], scale=1.0),
                    reads=[b_tmpn[j], b_modc], writes=[b_dst])

        def load_weight_bf16(dst_bf, b_dst, src_ap, stage_list, b_stage_list, k, shape_sl):
            i = k % len(stage_list)
            S.dma("sp", shape_sl(stage_list[i]), src_ap, writes=[b_stage_list[i]])
            S.op("pool", lambda e: e.tensor_copy(dst_bf, shape_sl(stage_list[i])),
                 reads=[b_stage_list[i]], writes=[b_dst])

        bgt = sb("bgt", [128, 40, 16])
        b_bgt = Buf("bgt", multi=True)
        hT_d = dscr("hT_d", [D, NT], BF16)
        hTd_v = hT_d.rearrange("(kc p) t -> p kc t", p=128)
        dn_o = dscr("dn_o", [2, 512, NT])
        gT = dscr("gT", [3072, NT], BF16)
        b_gT = Buf("gT", multi=True)
        b_dno = Buf("dn_o", multi=True)
        seqs = [(0, 32)] + [(32 + 2 * i, 2) for i in range(4)]
        pieces = [(i * 1024, 1024, i == 0, i == 3) for i in range(4)] + \
                 [(TS + i * 256, 256, True, True) for i in range(4)]

        def conv3(dst, b_dst, src, b_s, n, wcol):
            S.op("dve", lambda e: e.tensor_scalar(dst[:, 0:n], src[:, 0:n], pp[:, wcol:wcol + 1], None, ALU.mult),
                 reads=[b_s, b_pp], writes=[b_dst])
            for k in (1, 2):
                S.op("dve", lambda e: e.scalar_tensor_tensor(
                    dst[:, 0:n], src[:, k:k + n], pp[:, wcol + k:wcol + k + 1], dst[:, 0:n], ALU.mult, ALU.add),
                    reads=[b_s, b_pp, b_dst], writes=[b_dst])

        def load_halo(dst, b_dst, row0, t0, n, s0, s1):
            lo = t0 if s0 else t0 - 1
            hi = t0 + n if s1 else t0 + n + 1
            if s0:
                S.op("pool", lambda e: e.memset(dst[:, 0:1], 0.0), writes=[b_dst])
            if s1:
                S.op("pool", lambda e: e.memset(dst[:, n + 1:n + 2], 0.0), writes=[b_dst])
            off = 1 if s0 else 0
            S.dma("sp", dst[:, off:off + (hi - lo)], projT[row0:row0 + 128, lo:hi], reads=[b_proj], writes=[b_dst])

        for l in range(depth):
            S.dma("sp", pp[:], pp_in[l, :, :], writes=[b_pp])
            with ExitStack() as ph:
                wm = [sb(f"wm{i}", [128, 8, 768], ctx=ph) for i in range(2)]
                b_wm = [Buf(f"wm{i}") for i in range(2)]
                for pc in range(8):
                    i = pc % 2
                    S.dma("sp", wm[i][:], w_mod[l, :, pc * 768:(pc + 1) * 768].rearrange("(kc p) m -> p kc m", p=128),
                          writes=[b_wm[i]])
                    for o in range(6):
                        oc = pc * 6 + o
                        for kc in range(8):
                            S.op("pe", lambda e: e.matmul(
                                psum[0][:, oc * 2:oc * 2 + 2], wm[i][:, kc, o * 128:(o + 1) * 128], scv[:, kc, :],
                                start=(kc == 0), stop=(kc == 7)),
                                reads=[b_wm[i], b_scv], writes=[psb[0]], inc=(kc == 7))
                for g in range(2):
                    S.op("dve", lambda e: e.tensor_tensor(
                        modT[:, :, g], psum[0][:, g:96:2], pp[:, PP_BMOD:PP_BMOD + 48], ALU.add),
                        reads=[psb[0], b_pp], writes=[b_mod])
                for g in range(2):
                    for which, (sh_o, sc_o, gt_o, gpre, gpost) in enumerate(
                            ((0, 8, 16, PP_GPRE1, PP_GPOST1), (24, 32, 40, PP_GPRE2, PP_GPOST2))):
                        S.op("dve", lambda e: e.scalar_tensor_tensor(
                            modc[:, which * 3 + 0, :, g], modT[:, sc_o:sc_o + 8, g], 1.0, pp[:, gpre:gpre + 8],
                            ALU.add, ALU.mult), reads=[b_mod, b_pp], writes=[b_modc])
                        S.op("dve", lambda e: e.tensor_copy(
                            modc[:, which * 3 + 1, :, g], modT[:, sh_o:sh_o + 8, g]),
                            reads=[b_mod], writes=[b_modc])
                        S.op("dve", lambda e: e.tensor_tensor(
                            modc[:, which * 3 + 2, :, g], modT[:, gt_o:gt_o + 8, g], pp[:, gpost:gpost + 8],
                            ALU.mult), reads=[b_mod, b_pp], writes=[b_modc])
                S.barrier()

            lay = ExitStack()
            vsb = sb("vsb", [128, 44, 256], BF16, ctx=lay)
            b_vsb = Buf("vsb", multi=True)
            with ExitStack() as hsc:
                hT = sb("hT", [128, 8, NT], BF16, ctx=hsc)
                with ExitStack() as ph:
                    xb = [sb(f"xb{i}", [128, 8, TB], ctx=ph) for i in range(2)]
                    b_xb = [Buf(f"xb{i}") for i in range(2)]
                    for tb in range(NB):
                        i = tb % 2
                        S.dma("sp", xb[i][:], xT_v[:, :, tb * TB:(tb + 1) * TB], reads=[b_xT], writes=[b_xb[i]])
                        norm_mod_block(xb[i], b_xb[i], tb, 0, lambda kc: hT[:, kc, tb * TB:(tb + 1) * TB], b_hT)
                    S.barrier()

                with ExitStack() as ph:
                    wf = [sb(f"wf{i}", [128, 8, 256], ctx=ph) for i in range(2)]
                    b_wf = [Buf(f"wf{i}") for i in range(2)]
                    wb = [sb(f"wb{i}", [128, 8, 256], BF16, ctx=ph) for i in range(2)]
                    b_wb = [Buf(f"wb{i}") for i in range(2)]
                    stg = [sb(f"stg{i}", [128, TB], ctx=ph) for i in range(4)]
                    b_stg = [Buf(f"stg{i}") for i in range(4)]
                    stgb = [sb(f"stgb{i}", [128, TB], BF16, ctx=ph) for i in range(4)]
                    b_stgb = [Buf(f"stgb{i}") for i in range(4)]

                    def load_w(i, c0, M):
                        S.dma("sp", wf[i][:, :, 0:M], w_in[l, :, c0:c0 + M].rearrange("(kc p) m -> p kc m", p=128),
                              writes=[b_wf[i]])
                        S.op("pool", lambda e: e.tensor_copy(wb[i][:, :, 0:M], wf[i][:, :, 0:M]),
                             reads=[b_wf[i]], writes=[b_wb[i]])

                    load_w(0, C_AV, 256)
                    load_w(1, C_BETA, 16)
                    for a2 in range(2):
                        S.dma("sp", stg[a2][:].rearrange("p (a b) -> p a b", a=2),
                              cvv_in[l, a2 * 256:(a2 + 1) * 256, :].rearrange("(a p) m -> p a m", p=128), writes=[b_stg[a2]])
                        S.op("dve", lambda e: e.tensor_copy(vsb[:, 2 * a2:2 * a2 + 2, :].rearrange("p a b -> p (a b)"), stg[a2][:]),
                             reads=[b_stg[a2]], writes=[b_vsb])
                    for tt in range(0 if stop_after == 'P1' else NT // 128):
                        pb = tt % 4
                        si = tt % 4
                        for kc in range(8):
                            S.op("pe", lambda e: e.matmul(
                                psum[pb][:, 0:256], hT[:, kc, tt * 128:(tt + 1) * 128], wb[0][:, kc, :],
                                start=(kc == 0), stop=(kc == 7)),
                                reads=[b_wb[0], b_hT], writes=[psb[pb]], inc=(kc == 7))
                        evac(tt, vsb[:, 4 + tt, :], psum[pb][:, 0:256], [psb[pb]], [b_vsb])
                        if tt >= TS // 128 and "nv" not in skip:
                            p0 = (tt - TS // 128) * 128
                            S.op("dve", lambda e: e.tensor_copy(stg[si][:, 0:256], psum[pb][:, 0:256]),
                                 reads=[psb[pb]], writes=[b_stg[si]])
                            S.dma("sp", (y_p[p0:p0 + 128, 0:256] if "nvtest" in skip else nv_out[p0 // 256, l, p0 % 256:p0 % 256 + 128, :]), stg[si][:, 0:256],
                                  reads=[b_stg[si]], writes=[b_out])
                        pb2 = 4 + tt % 2
                        for kc in range(0 if "bgt" in skip else 8):
                            S.op("pe", lambda e: e.matmul(
                                psum[pb2][:, 0:16], hT[:, kc, tt * 128:(tt + 1) * 128], wb[1][:, kc, 0:16],
                                start=(kc == 0), stop=(kc == 7)),
                                reads=[b_wb[1], b_hT], writes=[psb[pb2]], inc=(kc == 7))
                        if "bgt" not in skip:
                            evac(tt + 1, bgt[:, tt, :], psum[pb2][:, 0:16], [psb[pb2]], [b_bgt])
                    chunks = [(c * 128, 128) for c in range(22)] + [(c * 128, 128) for c in range(24, 40)] + \
                             [(C_GATE + c * 128, 128) for c in range(24)]
                    if stop_after in ('P1', 'P2v'):
                        chunks = chunks[:1]
                    it = 0
                    load_w(0, chunks[0][0], 128)
                    for ci, (c0, M) in enumerate(chunks):
                        i = ci % 2
                        if ci + 1 < len(chunks):
                            load_w((ci + 1) % 2, chunks[ci + 1][0], 128)
                        is_gate = c0 >= C_GATE
                        for tb in range(NB):
                            pb = 4 + (it % 3)
                            si = it % 4
                            for kc in range(8):
                                S.op("pe", lambda e: e.matmul(
                                    psum[pb][0:M, :], wb[i][:, kc, 0:M], hT[:, kc, tb * TB:(tb + 1) * TB],
                                    start=(kc == 0), stop=(kc == 7)),
                                    reads=[b_wb[i], b_hT], writes=[psb[pb]], inc=(kc == 7))
                            if is_gate:
                                S.op("act", lambda e: e.activation(
                                    out=stgb[si][0:M, :], in_=psum[pb][0:M, :], func=AF.Sigmoid),
                                    reads=[psb[pb]], writes=[b_stgb[si]])
                                S.dma("pool", gT[c0 - C_GATE:c0 - C_GATE + M, tb * TB:(tb + 1) * TB], stgb[si][0:M, :],
                                      reads=[b_stgb[si]], writes=[b_gT])
                            else:
                                evac(it, stg[si][0:M, :], psum[pb][0:M, :], [psb[pb]], [b_stg[si]])
                                S.dma("pool", projT[c0:c0 + M, tb * TB:(tb + 1) * TB], stg[si][0:M, :],
                                      reads=[b_stg[si]], writes=[b_proj])
                            it += 1
                    S.barrier()
            if stop_after in ("P1", "P2v", "P2"):
                lay.close()
                break

            if "sconv" not in skip:
                with ExitStack() as ph:
                    cbuf = {nm: [sb(f"c_{nm}{i}", [128, 1026], ctx=ph) for i in range(2)] for nm in ("cb", "cc", "cx", "acc")}
                    b_cbuf = {nm: [Buf(f"c_{nm}{i}") for i in range(2)] for nm in cbuf}
                    cob = [sb(f"c_o{i}", [128, 1024], BF16, ctx=ph) for i in range(2)]
                    b_cob = [Buf(f"c_o{i}") for i in range(2)]
                    it = 0
                    for fc in range(4):
                        for (t0, n, s0, s1) in pieces:
                            i = it % 2
                            it += 1
                            S.dma("sp", cbuf["cb"][i][:, 0:n], projT[C_CB + fc * 128:C_CB + (fc + 1) * 128, t0:t0 + n],
                                  reads=[b_proj], writes=[b_cbuf["cb"][i]])
                            load_halo(cbuf["cc"][i], b_cbuf["cc"][i], C_CC + fc * 128, t0, n, s0, s1)
                            load_halo(cbuf["cx"][i], b_cbuf["cx"][i], C_CX + fc * 128, t0, n, s0, s1)
                            S.op("pool", lambda e: e.tensor_tensor(
                                cbuf["cc"][i][:, 0:n + 2], cbuf["cc"][i][:, 0:n + 2], cbuf["cx"][i][:, 0:n + 2], ALU.mult),
                                reads=[b_cbuf["cc"][i], b_cbuf["cx"][i]], writes=[b_cbuf["cc"][i]])
                            conv3(cbuf["acc"][i], b_cbuf["acc"][i], cbuf["cc"][i], b_cbuf["cc"][i], n, PP_CONVW + fc * 3)
                            S.op("pool", lambda e: e.tensor_tensor(
                                cob[i][:, 0:n], cbuf["acc"][i][:, 0:n], cbuf["cb"][i][:, 0:n], ALU.mult),
                                reads=[b_cbuf["acc"][i], b_cbuf["cb"][i]], writes=[b_cob[i]])
                            S.dma("pool", brT[fc * 128:(fc + 1) * 128, t0:t0 + n], cob[i][:, 0:n],
                                  reads=[b_cob[i]], writes=[b_br])
                    S.barrier()
            if stop_after == "P3a":
                lay.close()
                break

            if "attn" not in skip:
                with ExitStack() as ph:
                    kT = sb("kT", [128, 2, PAST + NT], BF16, ctx=ph)
                    b_kT = Buf("kT", multi=True)
                    ropet = [sb(f"a_rope{i}", [128, 2, TB], ctx=ph) for i in range(2)]
                    b_rope = [Buf(f"a_rope{i}") for i in range(2)]
                    raw = [sb(f"a_raw{i}", [128, TB], ctx=ph) for i in range(2)]
                    b_raw = [Buf(f"a_raw{i}") for i in range(2)]
                    kn = [sb(f"a_kn{i}", [128, TB], ctx=ph) for i in range(2)]
                    b_kn = [Buf(f"a_kn{i}") for i in range(2)]
                    t1 = [sb(f"a_t1{i}", [128, TB], ctx=ph) for i in range(2)]
                    b_t1 = [Buf(f"a_t1{i}") for i in range(2)]
                    t2 = [sb(f"a_t2{i}", [128, TB], ctx=ph) for i in range(2)]
                    b_t2 = [Buf(f"a_t2{i}") for i in range(2)]
                    qb = [sb(f"a_qb{i}", [128, TB], BF16, ctx=ph) for i in range(2)]
                    b_qb = [Buf(f"a_qb{i}") for i in range(2)]
                    pt = [sb(f"a_pt{i}", [128, TB], BF16, ctx=ph) for i in range(5)]
                    b_pt = [Buf(f"a_pt{i}") for i in range(5)]
                    accA = [sb(f"a_accA{i}", [128, TB], ctx=ph) for i in range(2)]
                    b_accA = [Buf(f"a_accA{i}") for i in range(2)]
                    accB = [sb(f"a_accB{i}", [128, TB], ctx=ph) for i in range(2)]
                    b_accB = [Buf(f"a_accB{i}") for i in range(2)]
                    rec = [sb(f"a_rec{i}", [128, TB], ctx=ph) for i in range(2)]
                    b_rec = [Buf(f"a_rec{i}") for i in range(2)]
                    ob = [sb(f"a_ob{i}", [128, TB], BF16, ctx=ph) for i in range(2)]
                    b_ob = [Buf(f"a_ob{i}") for i in range(2)]
                    ckst = sb("a_ck", [128, 4, 256], ctx=ph)
                    b_ckst = Buf("a_ck")
                    kst = [sb(f"a_kst{i}", [128, 4, 128], ctx=ph) for i in range(2)]
                    b_kst = [Buf(f"a_kst{i}") for i in range(2)]
                    PS_R = 6
                    cnt = {"n": 0}

                    def qk_prep(row0, t0, n, gcol, rope, dst_bf, b_dst):
                        i = cnt["n"] % 2
                        cnt["n"] += 1
                        S.dma("sp", raw[i][:, 0:n], projT[row0:row0 + 128, t0:t0 + n], reads=[b_proj], writes=[b_raw[i]])
                        if rope:
                            S.dma("sp", ropet[i][:, :, 0:n], rope_in[:, :, t0:t0 + n], writes=[b_rope[i]])
                        rms_rstd(lambda kc: raw[i][:, 0:n], b_raw[i], 1, n, 1.0 / 128)
                        S.op("dve", lambda e: e.scalar_tensor_tensor(
                            kn[i][:, 0:n], raw[i][:, 0:n], pp[:, gcol:gcol + 1], rstd[:, 0:n], ALU.mult, ALU.mult),
                            reads=[b_raw[i], b_pp, b_rstd], writes=[b_kn[i]])
                        if rope:
                            S.op("pe", lambda e: e.matmul(psum[PS_R][:, 0:n], pswapT, kn[i][:, 0:n], start=True, stop=True),
                                 reads=[b_kn[i], b_cst], writes=[psb[PS_R]])
                            S.op("pool", lambda e: e.tensor_tensor(t1[i][:, 0:n], kn[i][:, 0:n], ropet[i][:, 0, 0:n], ALU.mult),
                                 reads=[b_kn[i], b_rope[i]], writes=[b_t1[i]])
                            S.op("dve", lambda e: e.tensor_tensor(t2[i][:, 0:n], psum[PS_R][:, 0:n], ropet[i][:, 1, 0:n], ALU.mult),
                                 reads=[psb[PS_R], b_rope[i]], writes=[b_t2[i]])
                            S.op("pool", lambda e: e.tensor_tensor(dst_bf, t1[i][:, 0:n], t2[i][:, 0:n], ALU.add),
                                 reads=[b_t1[i], b_t2[i]], writes=[b_dst])
                        else:
                            S.op("act", lambda e: e.copy(dst_bf, kn[i][:, 0:n]), reads=[b_kn[i]], writes=[b_dst])
                        return kn[i], b_kn[i]

                    S.dma("sp", ckst[:], ck_in[l, :, :].rearrange("(a p) m -> p a m", p=128), writes=[b_ckst])
                    for j in range(2):
                        for a in range(4):
                            S.op("pe", lambda e: e.transpose(
                                psum[j][:, a * 128:(a + 1) * 128], ckst[:, a, j * 128:(j + 1) * 128], ident),
                                reads=[b_ckst, b_cst], writes=[psb[j]], inc=(a == 3))
                        evac(j, kT[:, j, 0:PAST], psum[j][:], [psb[j]], [b_kT])
                    for tb in range(NB):
                        t0 = tb * TB
                        is_s = tb < TS // TB
                        for j in range(2):
                            knt, b_knt = qk_prep(C_AK + j * 128, t0, TB, PP_GKN, is_s,
                                                 kT[:, j, PAST + t0:PAST + t0 + TB], b_kT)
                            if not is_s:
                                si = j
                                pbk = j
                                for a in range(4):
                                    S.op("pe", lambda e: e.transpose(
                                        psum[pbk][:, a * 128:(a + 1) * 128], knt[:, a * 128:(a + 1) * 128], ident),
                                        reads=[b_knt, b_cst], writes=[psb[pbk]], inc=(a == 3))
                                evac(j, kst[si][:].rearrange("p a b -> p (a b)"), psum[pbk][:], [psb[pbk]], [b_kst[si]])
                                for a in range(4):
                                    p0 = t0 - TS + a * 128
                                    S.dma("sp", nk_out[p0 // 256, l, p0 % 256:p0 % 256 + 128, j * 128:(j + 1) * 128],
                                          kst[si][:, a, :], reads=[b_kst[si]], writes=[b_out])
                    segs = [(0, TS, 0, list(range(0, 36)), TB)] + \
                           [(TS + i * 256, 256, PAST + TS + i * 256, [36 + 2 * i, 37 + 2 * i], 256) for i in range(4)]
                    scale = 128.0 ** -0.5
                    tasks = []
                    for (tq0, tqn, kcol0, vchunks, nq) in segs:
                        for h in range(8):
                            for qblk in range(tqn // nq):
                                tasks.append((tq0 == 0, h, tq0 + qblk * nq, nq, kcol0, vchunks))
                    itp = 0

                    def prep(k):
                        is_s, h, t0, nq, kcol0, vchunks = tasks[k]
                        qk_prep(C_AQ + h * 128, t0, nq, PP_GQN, is_s, qb[k % 2][:, 0:nq], b_qb[k % 2])

                    LA = 2
                    STB = (0, 1, 4)
                    PSS = 5
                    prep(0)
                    for k in range(len(tasks)):
                        if k + 1 < len(tasks):
                            prep(k + 1)
                        is_s, h, t0, nq, kcol0, vchunks = tasks[k]
                        j = h // 4
                        qi = k % 2
                        pso = 2 + qi
                        nch = len(vchunks)
                        base = itp
                        itp += nch
                        pe_sum_started = False
                        for step in range(nch + LA):
                            if step < nch:
                                ci = step
                                pst = STB[(base + ci) % 3]
                                pti = (base + ci) % 5
                                kc0 = kcol0 + ci * 128
                                S.op("pe", lambda e: e.matmul(
                                    psum[pst][:, 0:nq], kT[:, j, kc0:kc0 + 128], qb[qi][:, 0:nq], start=True, stop=True),
                                    reads=[b_kT, b_qb[qi]], writes=[psb[pst]])
                                S.op("act", lambda e: e.activation(
                                    out=pt[pti][:, 0:nq], in_=psum[pst][:, 0:nq], func=AF.Exp, scale=scale),
                                    reads=[psb[pst]], writes=[b_pt[pti]])
                                if False:
                                    ae = "dve"
                                    acc_t, b_acc_t = (accA[qi], b_accA[qi])
                                    if ci < 3:
                                        S.op(ae, lambda e: e.tensor_copy(acc_t[:, 0:nq], pt[pti][:, 0:nq]),
                                             reads=[b_pt[pti]], writes=[b_acc_t])
                                    else:
                                        S.op(ae, lambda e: e.tensor_tensor(acc_t[:, 0:nq], acc_t[:, 0:nq], pt[pti][:, 0:nq], ALU.add),
                                             reads=[b_pt[pti], b_acc_t], writes=[b_acc_t])
                            c1 = step - LA
                            if c1 >= 0:
                                pti = (base + c1) % 5
                                vch = vchunks[c1]
                                S.op("pe", lambda e: e.matmul(
                                    psum[pso][:, 0:nq], vsb[:, vch, j * 128:(j + 1) * 128], pt[pti][:, 0:nq],
                                    start=(c1 == 0), stop=(c1 == nch - 1)),
                                    reads=[b_vsb, b_pt[pti]], writes=[psb[pso]])
                                if True:
                                    S.op("pe", lambda e: e.matmul(
                                        psum[PSS][:, 0:nq], onesb[:], pt[pti][:, 0:nq],
                                        start=(not pe_sum_started), stop=(c1 == nch - 1)),
                                        reads=[b_cst, b_pt[pti]], writes=[psb[PSS]])
                                    pe_sum_started = True
                        S.op("dve", lambda e: e.reciprocal(rec[qi][:, 0:nq], psum[PSS][:, 0:nq]),
                             reads=[psb[PSS]], writes=[b_rec[qi]])
                        S.op("dve", lambda e: e.tensor_tensor(
                            ob[qi][:, 0:nq], psum[pso][:, 0:nq], rec[qi][:, 0:nq], ALU.mult),
                            reads=[psb[pso], b_rec[qi]], writes=[b_ob[qi]])
                        S.dma("pool", brT[512 + h * 128:512 + (h + 1) * 128, t0:t0 + nq], ob[qi][:, 0:nq],
                              reads=[b_ob[qi]], writes=[b_br])
                    S.barrier()
            lay.close()
            if stop_after == "P3b":
                break

            if "dn" not in skip:
                with ExitStack() as ph:
                    W8 = [128, 40, 8]
                    nbeta = sb("d_nbeta", W8, ctx=ph)
                    gtk = sb("d_g", W8, ctx=ph)
                    Gc = sb("d_G", W8, ctx=ph)
                    eG = sb("d_eG", W8, ctx=ph)
                    egl = sb("d_egl", W8, ctx=ph)
                    kdsc = sb("d_kdsc", W8, ctx=ph)
                    tA = sb("d_tA", W8, ctx=ph)
                    tB = sb("d_tB", W8, ctx=ph)
                    nega = sb("d_nega", [128, 8], ctx=ph)
                    b_g8 = Buf("d_gate8")
                    S.op("act", lambda e: e.activation(out=nbeta[:], in_=bgt[:, :, 0:8], func=AF.Sigmoid),
                         reads=[b_bgt], writes=[b_g8])
                    S.op("dve", lambda e: e.tensor_scalar(nbeta[:], nbeta[:], -1.0, None, ALU.mult), reads=[b_g8], writes=[b_g8])
                    S.op("act", lambda e: e.activation(out=nega[:], in_=pp[:, PP_ALOG:PP_ALOG + 8], func=AF.Exp),
                         reads=[b_pp], writes=[b_g8])
                    S.op("dve", lambda e: e.tensor_scalar(nega[:], nega[:], -1.0, None, ALU.mult), reads=[b_g8], writes=[b_g8])
                    for c in range(8):
                        S.op("dve", lambda e: e.tensor_scalar(tA[:, :, c], bgt[:, :, 8 + c], pp[:, PP_DTB + c:PP_DTB + c + 1], None, ALU.add),
                             reads=[b_bgt, b_pp, b_g8], writes=[b_g8])
                    S.op("act", lambda e: e.activation(out=tB[:], in_=tA[:], func=AF.Abs), reads=[b_g8], writes=[b_g8])
                    S.op("act", lambda e: e.activation(out=tB[:], in_=tB[:], func=AF.Exp, scale=-1.0), reads=[b_g8], writes=[b_g8])
                    S.op("act", lambda e: e.activation(out=tB[:], in_=tB[:], func=AF.Ln, bias=1.0, scale=1.0), reads=[b_g8], writes=[b_g8])
                    S.op("dve", lambda e: e.tensor_scalar(tA[:], tA[:], 0.0, None, ALU.max), reads=[b_g8], writes=[b_g8])
                    S.op("dve", lambda e: e.tensor_tensor(tA[:], tA[:], tB[:], ALU.add), reads=[b_g8], writes=[b_g8])
                    for c in range(8):
                        S.op("dve", lambda e: e.tensor_scalar(gtk[:, :, c], tA[:, :, c], nega[:, c:c + 1], None, ALU.mult),
                             reads=[b_g8], writes=[b_g8])
                    for tt in range(40):
                        pbk = tt % 2
                        for d in range(2):
                            S.op("pe", lambda e: e.matmul(psum[pbk][:, d * 4:d * 4 + 4], cst[:, 3 + d, :], gtk[:, tt, d * 4:d * 4 + 4],
                                                          start=True, stop=True), reads=[b_cst, b_g8], writes=[psb[pbk]])
                        S.op("pe", lambda e: e.matmul(psum[pbk][:, 8:16], ones, gtk[:, tt, :], start=True, stop=True),
                             reads=[b_cst, b_g8], writes=[psb[pbk]])
                        S.op("dve", lambda e: e.tensor_copy(Gc[:, tt, :], psum[pbk][:, 0:8]), reads=[psb[pbk]], writes=[b_g8])
                        S.op("act", lambda e: e.copy(egl[:, tt, :], psum[pbk][:, 8:16]), reads=[psb[pbk]], writes=[b_g8])
                    S.op("dve", lambda e: e.tensor_tensor(kdsc[:], egl[:], Gc[:], ALU.subtract), reads=[b_g8], writes=[b_g8])
                    S.op("act", lambda e: e.activation(out=kdsc[:], in_=kdsc[:], func=AF.Exp), reads=[b_g8], writes=[b_g8])
                    S.op("act", lambda e: e.activation(out=egl[:], in_=egl[:], func=AF.Exp), reads=[b_g8], writes=[b_g8])
                    S.op("act", lambda e: e.activation(out=eG[:], in_=Gc[:], func=AF.Exp), reads=[b_g8], writes=[b_g8])

                    qh = sb("d_qh", [128, NT], ctx=ph)
                    kh = sb("d_kh", [128, NT], ctx=ph)
                    vh = sb("d_vh", [128, NT], ctx=ph)
                    b_qkv = Buf("d_qkv", multi=True)
                    b_qraw = Buf("d_qraw", multi=True)
                    ktok = sb("d_ktok", [128, 40, 128], ctx=ph)
                    vtok = sb("d_vtok", [128, 40, 128], ctx=ph)
                    b_tok = Buf("d_tok", multi=True)
                    NU = 8
                    NB_PRE = 6
                    psl = {"n": 0, "o": 0}

                    def pslot():
                        i = psl["n"] % 24
                        psl["n"] += 1
                        return psum[i % 6][:, (i // 6) * 128:(i // 6 + 1) * 128], psb[i % 6]

                    evk = {"n": 0}

                    def evac2(out, in_, reads, writes):
                        evk["n"] += 1
                        evac(evk["n"], out, in_, reads, writes)

                    for h in range(4):
                        with ExitStack() as pa:
                            cin = [sb(f"d_cin{i}", [128, 1026], ctx=pa) for i in range(2)]
                            b_cin = [Buf(f"d_cin{i}") for i in range(2)]
                            cacc = [sb(f"d_cacc{i}", [128, 1024], ctx=pa) for i in range(2)]
                            b_cacc = [Buf(f"d_cacc{i}") for i in range(2)]
                            it = 0
                            for (dst, row0, wc, kind) in ((qh, C_DQ + h * 128, h, "q"), (kh, C_DK + h * 128, 4 + h, "k"),
                                                         (vh, C_DV + h * 128, 8 + h, "v")):
                                for (t0, n, s0, s1) in pieces:
                                    i = it % 2
                                    it += 1
                                    load_halo(cin[i], b_cin[i], row0, t0, n, s0, s1)
                                    conv3(cacc[i], b_cacc[i], cin[i], b_cin[i], n, PP_DNCONVW + wc * 3)
                                    S.op("act", lambda e: e.activation(out=dst[:, t0:t0 + n], in_=cacc[i][:, 0:n], func=AF.Silu),
                                         reads=[b_cacc[i]], writes=[b_qkv if kind == "v" else b_qraw])
                            for (dst, sc_) in ((qh, 128.0 ** -0.5), (kh, 1.0)):
                                for tb in range(NB):
                                    blk = dst[:, tb * TB:(tb + 1) * TB]
                                    rms_rstd(lambda kc: blk, b_qraw, 1, TB, 1.0)
                                    S.op("dve", lambda e: e.scalar_tensor_tensor(blk, blk, sc_, rstd[:, 0:TB], ALU.mult, ALU.mult),
                                         reads=[b_qraw, b_rstd], writes=[b_qkv])
                            for (src, dstt) in ((kh, ktok), (vh, vtok)):
                                for t4 in range(10):
                                    pbk = t4 % 2
                                    for a in range(4):
                                        tt = t4 * 4 + a
                                        S.op("pe", lambda e: e.transpose(psum[pbk][:, a * 128:(a + 1) * 128], src[:, tt * 128:(tt + 1) * 128], ident),
                                             reads=[b_qkv, b_cst], writes=[psb[pbk]], inc=(a == 3))
                                    evac2(dstt[:, t4 * 4:(t4 + 1) * 4, :].rearrange("p a b -> p (a b)"), psum[pbk][:], [psb[pbk]], [b_tok])
                            S.barrier()

                        with ExitStack() as pc:
                            ures = {nm: [sb(f"d_{nm}{i}", [128, 128], (BF16 if nm in ("RTb", "R", "x0") else F32), ctx=pc)
                                         for i in range(NU)]
                                    for nm in ("RT", "RTb", "R", "x0", "xT0", "attnT", "qgT", "kdec")}
                            b_ures = {nm: [Buf(f"d_{nm}{i}") for i in range(NU)] for nm in ures}
                            wpool = [[sb(f"d_wk{s_}_{i}", [128, 128], (F32 if i < 4 else BF16), ctx=pc) for i in range(8)]
                                     for s_ in range(NB_PRE)]
                            b_wpool = [[Buf(f"d_wk{s_}_{i}") for i in range(8)] for s_ in range(NB_PRE)]
                            spool = [[sb(f"d_sk{d}_{i}", [128, 128], ctx=pc) for i in range(4)] for d in range(2)]
                            b_spool = [[Buf(f"d_sk{d}_{i}") for i in range(4)] for d in range(2)]
                            Sst = [[sb(f"d_S{c}_{i}", [128, 128], ctx=pc) for i in range(2)] for c in range(2)]
                            b_Sst = [[Buf(f"d_S{c}_{i}") for i in range(2)] for c in range(2)]
                            ostg = [[sb(f"d_ostg{d}_{i}", [128, 128], ctx=pc) for i in range(2)] for d in range(2)]
                            b_ostg = [[Buf(f"d_ostg{d}_{i}") for i in range(2)] for d in range(2)]

                            def precompute(tt, d, u, slot):
                                c = d * 4 + h
                                tsl = slice(tt * 128, (tt + 1) * 128)
                                tri = cst[:, 3 + d, :]
                                wn = {"n": 0}

                                def wtile():
                                    i = wn["n"]
                                    wn["n"] += 1
                                    i = i if i < 4 else 4 + (i - 4) % 4
                                    return wpool[slot][i], b_wpool[slot][i]
                                pn_ = {"n": 0}

                                def pslot():
                                    i = pn_["n"] % 4
                                    pn_["n"] += 1
                                    return psum[slot][:, i * 128:(i + 1) * 128], psb[slot]
                                xT0, b_xT0 = ures["xT0"][u], b_ures["xT0"][u]
                                x0, b_x0 = ures["x0"][u], b_ures["x0"][u]
                                RTf, b_RTf = ures["RT"][u], b_ures["RT"][u]
                                RT, b_RT = ures["RTb"][u], b_ures["RTb"][u]
                                R, b_R = ures["R"][u], b_ures["R"][u]
                                pg, bpg = pslot()
                                S.op("pe", lambda e: e.matmul(pg, gtk[:, tt, c:c + 1].to_broadcast([128, 128]), tri, start=True, stop=True),
                                     reads=[b_g8, b_cst], writes=[bpg])
                                pk, bpk = pslot()
                                S.op("pe", lambda e: e.matmul(pk, kh[:, tsl], kh[:, tsl], start=True, stop=True),
                                     reads=[b_qkv], writes=[bpk])
                                pq, bpq = pslot()
                                S.op("pe", lambda e: e.matmul(pq, kh[:, tsl], qh[:, tsl], start=True, stop=True),
                                     reads=[b_qkv], writes=[bpq])
                                S.op("act", lambda e: e.activation(out=ures["kdec"][u][:], in_=ktok[:, tt, :], func=AF.Identity,
                                                                   scale=kdsc[:, tt, c:c + 1]),
                                     reads=[b_tok, b_g8], writes=[b_ures["kdec"][u]])
                                yield
                                dm, b_dm = wtile()
                                S.op("dve", lambda e: e.tensor_scalar(dm[:], pg, Gc[:, tt, c:c + 1], 0.0, ALU.subtract, ALU.min),
                                     reads=[bpg, b_g8], writes=[b_dm])
                                egb, b_egb = wtile()
                                S.op("act", lambda e: e.activation(out=egb[:], in_=pg, func=AF.Exp), reads=[bpg], writes=[b_egb])
                                yield
                                S.op("act", lambda e: e.activation(out=dm[:], in_=dm[:], func=AF.Exp), reads=[b_dm], writes=[b_dm])
                                S.op("pool", lambda e: e.tensor_tensor(ures["qgT"][u][:], qh[:, tsl], egb[:], ALU.mult),
                                     reads=[b_qkv, b_egb], writes=[b_ures["qgT"][u]])
                                yield
                                ei, b_ei = wtile()
                                S.op("pool", lambda e: e.tensor_tensor(ei[:], dm[:], tri, ALU.mult), reads=[b_dm, b_cst], writes=[b_ei])
                                yield
                                es_, b_es = wtile()
                                S.op("pool", lambda e: e.tensor_tensor(es_[:], ei[:], ident, ALU.subtract), reads=[b_ei, b_cst], writes=[b_es])
                                S.op("dve", lambda e: e.tensor_tensor(ures["attnT"][u][:], pq, ei[:], ALU.mult),
                                     reads=[bpq, b_ei], writes=[b_ures["attnT"][u]])
                                yield
                                S.op("dve", lambda e: e.scalar_tensor_tensor(xT0[:], pk, nbeta[:, tt, c:c + 1], es_[:], ALU.mult, ALU.mult),
                                     reads=[bpk, b_g8, b_es], writes=[b_xT0])
                                yield
                                px, bpx = pslot()
                                S.op("pe", lambda e: e.transpose(px, xT0[:], ident), reads=[b_xT0, b_cst], writes=[bpx])
                                XT, b_XT = wtile()
                                S.op("pool", lambda e: e.tensor_tensor(XT[:], xT0[:], cst[:, 5, :], ALU.mult), reads=[b_xT0, b_cst], writes=[b_XT])
                                yield
                                evac2(x0[:], px, [bpx], [b_x0])
                                S.op("pool", lambda e: e.tensor_tensor(RT[:], XT[:], ident, ALU.add), reads=[b_XT, b_cst], writes=[b_RT])
                                yield
                                X, b_X = wtile()
                                S.op("pool", lambda e: e.tensor_tensor(X[:], x0[:], cst[:, 5, :], ALU.mult), reads=[b_x0, b_cst], writes=[b_X])
                                yield
                                S.op("pool", lambda e: e.tensor_tensor(R[:], X[:], ident, ALU.add), reads=[b_X, b_cst], writes=[b_R])
                                for lev in range(3):
                                    pn, bpn = pslot()
                                    S.op("pe", lambda e: e.matmul(pn, XT[:], X[:], start=True, stop=True), reads=[b_XT, b_X], writes=[bpn])
                                    pnt, bpnt = pslot()
                                    S.op("pe", lambda e: e.matmul(pnt, X[:], XT[:], start=True, stop=True), reads=[b_XT, b_X], writes=[bpnt])
                                    yield
                                    Xn, b_Xn = wtile()
                                    evac2(Xn[:], pn, [bpn], [b_Xn])
                                    XTn, b_XTn = wtile()
                                    evac2(XTn[:], pnt, [bpnt], [b_XTn])
                                    yield
                                    pr, bpr = pslot()
                                    S.op("pe", lambda e: e.matmul(pr, Xn[:], RT[:], start=True, stop=True), reads=[b_Xn, b_RT], writes=[bpr])
                                    pr2, bpr2 = pslot()
                                    S.op("pe", lambda e: e.matmul(pr2, XTn[:], R[:], start=True, stop=True), reads=[b_XTn, b_R], writes=[bpr2])
                                    yield
                                    S.op("dve", lambda e: e.tensor_tensor(RT[:], pr, RT[:], ALU.add), reads=[bpr, b_RT], writes=[b_RT])
                                    S.op("dve", lambda e: e.tensor_tensor(R[:], pr2, R[:], ALU.add), reads=[bpr2, b_R], writes=[b_R])
                                    X, b_X, XT, b_XT = Xn, b_Xn, XTn, b_XTn
                                    yield
                                for li in range(3):
                                    last = li == 2
                                    msk = cst[:, 6 + li, :]
                                    YT, b_YT = wtile()
                                    S.op("pool", lambda e: e.tensor_tensor(YT[:], x0[:], msk, ALU.mult), reads=[b_x0, b_cst], writes=[b_YT])
                                    if not last:
                                        Y_, b_Y = wtile()
                                        S.op("pool", lambda e: e.tensor_tensor(Y_[:], xT0[:], msk, ALU.mult), reads=[b_xT0, b_cst], writes=[b_Y])
                                    yield
                                    pp_, bpp_ = pslot()
                                    S.op("pe", lambda e: e.matmul(pp_, YT[:], RT[:], start=True, stop=True), reads=[b_YT, b_RT], writes=[bpp_])
                                    if not last:
                                        pq_, bpq_ = pslot()
                                        S.op("pe", lambda e: e.matmul(pq_, Y_[:], R[:], start=True, stop=True), reads=[b_Y, b_R], writes=[bpq_])
                                    yield
                                    P_, b_P = wtile()
                                    evac2(P_[:], pp_, [bpp_], [b_P])
                                    if not last:
                                        Q_, b_Q = wtile()
                                        evac2(Q_[:], pq_, [bpq_], [b_Q])
                                    yield
                                    pa_, bpa_ = pslot()
                                    S.op("pe", lambda e: e.matmul(pa_, R[:], P_[:], start=True, stop=True), reads=[b_R, b_P], writes=[bpa_])
                                    if not last:
                                        pb_, bpb_ = pslot()
                                        S.op("pe", lambda e: e.matmul(pb_, RT[:], Q_[:], start=True, stop=True), reads=[b_RT, b_Q], writes=[bpb_])
                                    yield
                                    if not last:
                                        S.op("dve", lambda e: e.tensor_tensor(RT[:], pa_, RT[:], ALU.add), reads=[bpa_, b_RT], writes=[b_RT])
                                        S.op("dve", lambda e: e.tensor_tensor(R[:], pb_, R[:], ALU.add), reads=[bpb_, b_R], writes=[b_R])
                                    else:
                                        S.op("dve", lambda e: e.tensor_tensor(RTf[:], pa_, RT[:], ALU.add), reads=[bpa_, b_RT], writes=[b_RTf])
                                    yield

                            units = []
                            for si_, (tile0, ntile) in enumerate(seqs):
                                for s_ in range(ntile):
                                    for d in range(2):
                                        tt = tile0 + s_ if d == 0 else tile0 + ntile - 1 - s_
                                        units.append((tt, d, si_, s_))
                            unit_of = {(si_, s_, d): n for n, (tt, d, si_, s_) in enumerate(units)}
                            pre_done = [False] * len(units)
                            scan_done = [False] * len(units)

                            def chain(d):
                                sn = {"n": 0}

                                def stile():
                                    i = sn["n"] % 4
                                    sn["n"] += 1
                                    return spool[d][i], b_spool[d][i]
                                sp_ = {"n": 0}

                                def pslot():
                                    i = sp_["n"] % 4
                                    sp_["n"] += 1
                                    return psum[6 + d][:, i * 128:(i + 1) * 128], psb[6 + d]
                                c = d * 4 + h
                                for si_, (tile0, ntile) in enumerate(seqs):
                                    if si_ == 0:
                                        S.dma("sp", Sst[d][0][:], sdn_in[l, d, h, :, :], writes=[b_Sst[d][0]])
                                    else:
                                        S.op("pool", lambda e: e.memset(Sst[d][0][:], 0.0), writes=[b_Sst[d][0]])
                                    for s_ in range(ntile):
                                        n = unit_of[(si_, s_, d)]
                                        while not pre_done[n]:
                                            yield
                                        tt = units[n][0]
                                        u = n % NU
                                        tsl = slice(tt * 128, (tt + 1) * 128)
                                        par = s_ % 2
                                        S_old, b_So = Sst[d][par], b_Sst[d][par]
                                        S_new, b_Sn = Sst[d][1 - par], b_Sst[d][1 - par]
                                        p1, bp1 = pslot()
                                        S.op("pe", lambda e: e.matmul(p1, kh[:, tsl], S_old[:], start=True, stop=True), reads=[b_qkv, b_So], writes=[bp1])
                                        yield
                                        nr, b_nr = stile()
                                        S.op("dve", lambda e: e.scalar_tensor_tensor(nr[:], p1, eG[:, tt, c:c + 1], vtok[:, tt, :], ALU.mult, ALU.subtract),
                                             reads=[bp1, b_g8, b_tok], writes=[b_nr])
                                        yield
                                        p2, bp2 = pslot()
                                        S.op("pe", lambda e: e.matmul(p2, ures["RT"][u][:], nr[:], start=True, stop=True),
                                             reads=[b_ures["RT"][u], b_nr], writes=[bp2])
                                        yield
                                        vn, b_vn = stile()
                                        S.op("act", lambda e: e.activation(out=vn[:], in_=p2, func=AF.Identity, scale=nbeta[:, tt, c:c + 1]),
                                             reads=[bp2, b_g8], writes=[b_vn])
                                        yield
                                        po, bpo = pslot()
                                        S.op("pe", lambda e: e.matmul(po, S_old[:], ures["qgT"][u][:], start=True, stop=False),
                                             reads=[b_So, b_ures["qgT"][u]], writes=[bpo], inc=False)
                                        S.op("pe", lambda e: e.matmul(po, vn[:], ures["attnT"][u][:], start=False, stop=True),
                                             reads=[b_vn, b_ures["attnT"][u]], writes=[bpo])
                                        p3, bp3 = pslot()
                                        S.op("pe", lambda e: e.matmul(p3, ures["kdec"][u][:], vn[:], start=True, stop=True),
                                             reads=[b_ures["kdec"][u], b_vn], writes=[bp3])
                                        yield
                                        oi = s_ % 2
                                        evac2(ostg[d][oi][:], po, [bpo], [b_ostg[d][oi]])
                                        S.dma("pool", dn_o[d, h * 128:(h + 1) * 128, tsl], ostg[d][oi][:], reads=[b_ostg[d][oi]], writes=[b_dno])
                                        S.op("dve", lambda e: e.scalar_tensor_tensor(S_new[:], S_old[:], egl[:, tt, c:c + 1], p3, ALU.mult, ALU.add),
                                             reads=[b_So, b_g8, bp3], writes=[b_Sn])
                                        scan_done[n] = True
                                        yield
                                    if si_ > 0:
                                        S.dma("sp", ns_out[si_ - 1, l, d, h, :, :], Sst[d][ntile % 2][:],
                                              reads=[b_Sst[d][ntile % 2]], writes=[b_out])

                            chains = [chain(0), chain(1)]
                            active = [None] * NB_PRE
                            next_pre = 0
                            while chains or any(a is not None for a in active) or next_pre < len(units):
                                for slot in range(NB_PRE):
                                    if active[slot] is None and next_pre < len(units):
                                        n = next_pre
                                        if n < NU or scan_done[n - NU]:
                                            tt, d, si_, s_ = units[n]
                                            active[slot] = (precompute(tt, d, n % NU, slot), n)
                                            next_pre += 1
                                    if active[slot] is not None:
                                        g_, n = active[slot]
                                        try:
                                            next(g_)
                                        except StopIteration:
                                            pre_done[n] = True
                                            active[slot] = None
                                for cg in list(chains):
                                    try:
                                        next(cg)
                                    except StopIteration:
                                        chains.remove(cg)
                            S.barrier()

                    S.barrier()
                with ExitStack() as ph:
                    of_ = [sb(f"e_of{i}", [128, TB], ctx=ph) for i in range(2)]
                    b_of = [Buf(f"e_of{i}") for i in range(2)]
                    ob_ = [sb(f"e_ob{i}", [128, TB], ctx=ph) for i in range(2)]
                    b_ob_ = [Buf(f"e_ob{i}") for i in range(2)]
                    zt = [sb(f"e_z{i}", [128, TB], ctx=ph) for i in range(2)]
                    b_zt = [Buf(f"e_z{i}") for i in range(2)]
                    yo_ = [sb(f"e_y{i}", [128, TB], BF16, ctx=ph) for i in range(2)]
                    b_yo_ = [Buf(f"e_y{i}") for i in range(2)]
                    it = 0
                    for h in range(4):
                        for tb in range(NB):
                            i = it % 2
                            it += 1
                            tsl = slice(tb * TB, (tb + 1) * TB)
                            S.dma("sp", of_[i][:], dn_o[0, h * 128:(h + 1) * 128, tsl], reads=[b_dno], writes=[b_of[i]])
                            S.dma("sp", ob_[i][:], dn_o[1, h * 128:(h + 1) * 128, tsl], reads=[b_dno], writes=[b_ob_[i]])
                            S.dma("sp", zt[i][:], projT[C_DZ + h * 128:C_DZ + (h + 1) * 128, tsl], reads=[b_proj], writes=[b_zt[i]])
                            S.op("pool", lambda e: e.tensor_tensor(of_[i][:], of_[i][:], ob_[i][:], ALU.add),
                                 reads=[b_of[i], b_ob_[i]], writes=[b_of[i]])
                            S.op("act", lambda e: e.activation(out=zt[i][:], in_=zt[i][:], func=AF.Silu), reads=[b_zt[i]], writes=[b_zt[i]])
                            rms_rstd(lambda kc: of_[i][:], b_of[i], 1, TB, 1.0 / 128)
                            S.op("dve", lambda e: e.scalar_tensor_tensor(
                                ob_[i][:], of_[i][:], pp[:, PP_DNG:PP_DNG + 1], rstd[:], ALU.mult, ALU.mult),
                                reads=[b_of[i], b_pp, b_rstd], writes=[b_ob_[i]])
                            S.op("pool", lambda e: e.tensor_tensor(yo_[i][:], ob_[i][:], zt[i][:], ALU.mult),
                                 reads=[b_ob_[i], b_zt[i]], writes=[b_yo_[i]])
                            S.dma("pool", brT[1536 + h * 128:1536 + (h + 1) * 128, tsl], yo_[i][:], reads=[b_yo_[i]], writes=[b_br])
                    S.barrier()
            if stop_after == "P3c":
                break

            with ExitStack() as ph:
                wpa = sb("wpa", [128, 4, D], BF16, ctx=ph)
                wpb = sb("wpb", [128, 8, D], BF16, ctx=ph)
                wpc = sb("wpc", [128, 4, D], BF16, ctx=ph)
                wo = sb("wo", [128, 8, D], BF16, ctx=ph)
                b_w4 = Buf("w4", multi=True)
                wst = [sb(f"wst{i}", [128, 1, D], ctx=ph) for i in range(2)]
                b_wst = [Buf(f"wst{i}") for i in range(2)]
                k = 0
                for (dst, src, nk_) in ((wpa, w_pa, 4), (wpb, w_pb, 8), (wpc, w_pc, 4), (wo, w_o, 8)):
                    for kc in range(nk_):
                        load_weight_bf16(dst[:, kc:kc + 1, :], b_w4,
                                         src[l, kc * 128:(kc + 1) * 128, :].rearrange("(kc p) m -> p kc m", p=128),
                                         wst, b_wst, k, lambda t: t[:])
                        k += 1
                brb = sb("brb", [128, 16, TB], BF16, ctx=ph)
                b_brb = Buf("brb")
                gts = [sb(f"gts{i}", [128, 3, TB], BF16, ctx=ph) for i in range(2)]
                b_gts = [Buf(f"gts{i}") for i in range(2)]
                xb4 = sb("xb4", [128, 8, TB], ctx=ph)
                b_xb4 = Buf("xb4")
                mixb = sb("mixb", [128, 8, TB], BF16, ctx=ph)
                b_mixb = Buf("mixb", multi=True)
                m2 = sb("m2", [128, 8, TB], ctx=ph)
                b_m2 = Buf("m2", multi=True)
                h2s = sb("h2s", [128, 8, TB], BF16, ctx=ph)
                b_h2s = Buf("h2s", multi=True)
                u1 = [sb(f"u1{i}", [128, TB], ctx=ph) for i in range(2)]
                b_u1 = [Buf(f"u1{i}") for i in range(2)]
                u2 = [sb(f"u2{i}", [128, TB], ctx=ph) for i in range(2)]
                b_u2 = [Buf(f"u2{i}") for i in range(2)]
                for tb in range(NB):
                    g = 0 if tb < TS // TB else 1
                    tsl = slice(tb * TB, (tb + 1) * TB)
                    S.dma("sp", brb[:], brT.rearrange("(kc p) t -> p kc t", p=128)[:, :, tsl],
                          reads=[b_br], writes=[b_brb])
                    S.dma("sp", xb4[:], xT_v[:, :, tsl], reads=[b_xT], writes=[b_xb4])
                    for oc in range(8):
                        gi = oc % 2
                        S.dma("sp", gts[gi][:],
                              gT[:, tsl].rearrange("(a c p) t -> p a c t", a=3, p=128)[:, :, oc, :],
                              reads=[b_gT], writes=[b_gts[gi]])
                        osl = slice(oc * 128, (oc + 1) * 128)
                        for bi, (wt, k0, nk_) in enumerate(((wpa, 0, 4), (wpb, 4, 8), (wpc, 12, 4))):
                            for kc in range(nk_):
                                S.op("pe", lambda e: e.matmul(
                                    psum[bi][:], wt[:, kc, osl], brb[:, k0 + kc, :], start=(kc == 0), stop=(kc == nk_ - 1)),
                                    reads=[b_w4, b_brb], writes=[psb[bi]], inc=(kc == nk_ - 1))
                        ui = oc % 2
                        S.op("dve", lambda e: e.tensor_tensor(u1[ui][:], psum[0][:], gts[gi][:, 0, :], ALU.mult),
                             reads=[psb[0], b_gts[gi]], writes=[b_u1[ui]])
                        S.op("dve", lambda e: e.tensor_tensor(u2[ui][:], psum[1][:], gts[gi][:, 1, :], ALU.mult),
                             reads=[psb[1], b_gts[gi]], writes=[b_u2[ui]])
                        S.op("pool", lambda e: e.tensor_tensor(u1[ui][:], u1[ui][:], u2[ui][:], ALU.add),
                             reads=[b_u1[ui], b_u2[ui]], writes=[b_u1[ui]])
                        S.op("dve", lambda e: e.tensor_tensor(u2[ui][:], psum[2][:], gts[gi][:, 2, :], ALU.mult),
                             reads=[psb[2], b_gts[gi]], writes=[b_u2[ui]])
                        S.op("pool", lambda e: e.tensor_tensor(mixb[:, oc, :], u1[ui][:], u2[ui][:], ALU.add),
                             reads=[b_u1[ui], b_u2[ui]], writes=[b_mixb])
                    for oc in range(8):
                        pb = 3 + oc % 3
                        osl = slice(oc * 128, (oc + 1) * 128)
                        for kc in range(8):
                            S.op("pe", lambda e: e.matmul(
                                psum[pb][:], wo[:, kc, osl], mixb[:, kc, :], start=(kc == 0), stop=(kc == 7)),
                                reads=[b_w4, b_mixb], writes=[psb[pb]], inc=(kc == 7))
                        evac(oc, m2[:, oc, :], psum[pb][:], [psb[pb]], [b_m2])
                    rms_rstd(lambda kc: m2[:, kc, :], b_m2, 8, TB, 1.0 / D)
                    for kc in range(8):
                        ui = kc % 2
                        S.op("dve", lambda e: e.scalar_tensor_tensor(
                            u1[ui][:], m2[:, kc, :], modc[:, 2, kc, g:g + 1], rstd[:], ALU.mult, ALU.mult),
                            reads=[b_m2, b_modc, b_rstd], writes=[b_u1[ui]])
                        S.op("pool", lambda e: e.tensor_tensor(
                            xb4[:, kc, :], xb4[:, kc, :], u1[ui][:], ALU.add),
                            reads=[b_xb4, b_u1[ui]], writes=[b_xb4])
                    S.dma("pool", xT_v[:, :, tsl], xb4[:], reads=[b_xb4], writes=[b_xT])
                    norm_mod_block(xb4, b_xb4, tb, 1, lambda kc: h2s[:, kc, :], b_h2s)
                    S.dma("pool", hTd_v[:, :, tsl], h2s[:], reads=[b_h2s], writes=[b_hTd])
                S.barrier()
            if stop_after == "P4":
                break

            with ExitStack() as ph:
                hT = sb("hT2", [128, 8, NT], BF16, ctx=ph)
                b_hT2 = Buf("hT2", multi=True)
                for tb in range(NB):
                    S.dma("sp", hT[:, :, tb * TB:(tb + 1) * TB], hTd_v[:, :, tb * TB:(tb + 1) * TB], reads=[b_hTd], writes=[b_hT2])
                wf = [sb(f"f_wf{i}", [128, 8, 256], ctx=ph) for i in range(2)]
                b_wf = [Buf(f"f_wf{i}") for i in range(2)]
                wb = [sb(f"f_wb{i}", [128, 8, 256], BF16, ctx=ph) for i in range(2)]
                b_wb = [Buf(f"f_wb{i}") for i in range(2)]
                sg = [sb(f"f_sg{i}", [128, TB], ctx=ph) for i in range(2)]
                b_sg = [Buf(f"f_sg{i}") for i in range(2)]
                ao = [sb(f"f_ao{i}", [128, TB], BF16, ctx=ph) for i in range(3)]
                b_ao = [Buf(f"f_ao{i}") for i in range(3)]

                def load_wf(i, fc):
                    S.dma("sp", wf[i][:, :, 0:128], w_gate[l, :, fc * 128:(fc + 1) * 128].rearrange("(kc p) m -> p kc m", p=128),
                          writes=[b_wf[i]])
                    S.dma("sp", wf[i][:, :, 128:256], w_up[l, :, fc * 128:(fc + 1) * 128].rearrange("(kc p) m -> p kc m", p=128),
                          writes=[b_wf[i]])
                    S.op("pool", lambda e: e.tensor_copy(wb[i][:], wf[i][:]), reads=[b_wf[i]], writes=[b_wb[i]])
                it = 0
                load_wf(0, 0)
                for fc in range(22):
                    i = fc % 2
                    if fc + 1 < 22:
                        load_wf((fc + 1) % 2, fc + 1)
                    for tb in range(NB):
                        pg, pu = (it % 2) * 2, (it % 2) * 2 + 1
                        si, ai = it % 2, it % 3
                        it += 1
                        tsl = slice(tb * TB, (tb + 1) * TB)
                        for (pbk, c0) in ((pg, 0), (pu, 128)):
                            for kc in range(8):
                                S.op("pe", lambda e: e.matmul(
                                    psum[pbk][:], wb[i][:, kc, c0:c0 + 128], hT[:, kc, tsl], start=(kc == 0), stop=(kc == 7)),
                                    reads=[b_wb[i], b_hT2], writes=[psb[pbk]], inc=(kc == 7))
                        S.op("act", lambda e: e.activation(out=sg[si][:], in_=psum[pg][:], func=AF.Silu),
                             reads=[psb[pg]], writes=[b_sg[si]])
                        S.op("dve", lambda e: e.tensor_tensor(ao[ai][:], psum[pu][:], sg[si][:], ALU.mult),
                             reads=[psb[pu], b_sg[si]], writes=[b_ao[ai]])
                        S.dma("pool", actT[fc * 128:(fc + 1) * 128, tsl], ao[ai][:], reads=[b_ao[ai]], writes=[b_actT])
                S.barrier()

            with ExitStack() as ph:
                wd = sb("wd", [128, 22, D], BF16, ctx=ph)
                b_wd = Buf("wd", multi=True)
                wst = [sb(f"d_wst{i}", [128, 1, D], ctx=ph) for i in range(2)]
                b_wst = [Buf(f"d_wst{i}") for i in range(2)]
                for kc in range(22):
                    load_weight_bf16(wd[:, kc:kc + 1, :], b_wd,
                                     w_down[l, kc * 128:(kc + 1) * 128, :].rearrange("(kc p) m -> p kc m", p=128),
                                     wst, b_wst, kc, lambda t: t[:])
                ab = [sb(f"d_ab{i}", [128, 22, TB], BF16, ctx=ph) for i in range(2)]
                b_ab = [Buf(f"d_ab{i}") for i in range(2)]
                xb6 = [sb(f"xb6{i}", [128, 8, TB], ctx=ph) for i in range(2)]
                b_xb6 = [Buf(f"xb6{i}") for i in range(2)]
                f2 = sb("f2", [128, 8, TB], ctx=ph)
                b_f2 = Buf("f2", multi=True)
                u1 = [sb(f"d_u1{i}", [128, TB], ctx=ph) for i in range(2)]
                b_u1 = [Buf(f"d_u1{i}") for i in range(2)]
                for tb in range(NB):
                    i = tb % 2
                    g = 0 if tb < TS // TB else 1
                    tsl = slice(tb * TB, (tb + 1) * TB)
                    S.dma("sp", ab[i][:], actT.rearrange("(kc p) t -> p kc t", p=128)[:, :, tsl],
                          reads=[b_actT], writes=[b_ab[i]])
                    S.dma("sp", xb6[i][:], xT_v[:, :, tsl], reads=[b_xT], writes=[b_xb6[i]])
                    for oc in range(8):
                        pb = oc % 4
                        osl = slice(oc * 128, (oc + 1) * 128)
                        for kc in range(22):
                            S.op("pe", lambda e: e.matmul(
                                psum[pb][:], wd[:, kc, osl], ab[i][:, kc, :], start=(kc == 0), stop=(kc == 21)),
                                reads=[b_wd, b_ab[i]], writes=[psb[pb]], inc=(kc == 21))
                        evac(oc, f2[:, oc, :], psum[pb][:], [psb[pb]], [b_f2])
                    rms_rstd(lambda kc: f2[:, kc, :], b_f2, 8, TB, 1.0 / D)
                    for kc in range(8):
                        ui = kc % 2
                        S.op("dve", lambda e: e.scalar_tensor_tensor(
                            u1[ui][:], f2[:, kc, :], modc[:, 5, kc, g:g + 1], rstd[:], ALU.mult, ALU.mult),
                            reads=[b_f2, b_modc, b_rstd], writes=[b_u1[ui]])
                        S.op("pool", lambda e: e.tensor_tensor(
                            xb6[i][:, kc, :], xb6[i][:, kc, :], u1[ui][:], ALU.add),
                            reads=[b_xb6[i], b_u1[ui]], writes=[b_xb6[i]])
                    S.dma("pool", xT_v[:, :, tsl], xb6[i][:], reads=[b_xb6[i]], writes=[b_xT])
                S.barrier()
            if stop_after == "P6":
                break

        S.barrier()
        with ExitStack() as ph:
            xi = [sb(f"fx{i}", [128, 8, 128], ctx=ph) for i in range(2)]
            b_xi = [Buf(f"fx{i}") for i in range(2)]
            yo = [sb(f"fy{i}", [128, D], ctx=ph) for i in range(2)]
            b_yo = [Buf(f"fy{i}") for i in range(2)]
            for tt in range(NT // 128):
                i = tt % 2
                S.dma("sp", xi[i][:], xT_v[:, :, tt * 128:(tt + 1) * 128], reads=[b_xT], writes=[b_xi[i]])
                for half in range(2):
                    pb = (tt * 2 + half) % 8
                    for j in range(4):
                        kc = half * 4 + j
                        S.op("pe", lambda e, kc=kc, j=j, pb=pb, i=i: e.transpose(
                            psum[pb][:, j * 128:(j + 1) * 128], xi[i][:, kc, :], ident),
                            reads=[b_xi[i], b_cst], writes=[psb[pb]], inc=(j == 3))
                    evac(half, yo[i][:, half * 512:(half + 1) * 512], psum[pb][:], [psb[pb]], [b_yo[i]])
                dst = y_s[tt * 128:(tt + 1) * 128, :] if tt < TS // 128 else \
                    y_p[(tt - TS // 128) * 128:(tt - TS // 128 + 1) * 128, :]
                S.dma("sp", dst, yo[i][:], reads=[b_yo[i]], writes=[b_out])
        S.barrier()
        S.finish()
    print(f"[build] instructions={S.nins} waits={S.nwait}", flush=True)
    return nc


def _consts():
    c = np.zeros((128, NCONST, 128), np.float32)
    c[:, 0, :] = np.eye(128, dtype=np.float32)
    c[:, 1, :] = 1.0
    for i in range(64):
        c[2 * i + 1, 2, 2 * i] = -1.0
        c[2 * i, 2, 2 * i + 1] = 1.0
    p = np.arange(128)
    c[:, 3, :] = (p[:, None] <= p[None, :]).astype(np.float32)
    c[:, 4, :] = (p[:, None] >= p[None, :]).astype(np.float32)
    blk = lambda s_: (p[:, None] // s_ == p[None, :] // s_)
    c[:, 5, :] = blk(16).astype(np.float32)
    c[:, 6, :] = (blk(32) & ~blk(16)).astype(np.float32)
    c[:, 7, :] = (blk(64) & ~blk(32)).astype(np.float32)
    c[:, 8, :] = (~blk(64)).astype(np.float32)
    return c


def _rope_tables():
    rows = TS // 64
    row_id = np.repeat(np.arange(rows, dtype=np.float32), 64)
    col_id = np.tile(np.arange(64, dtype=np.float32), rows)
    n_freq = 32
    inv_freq = (np.float32(10000.0) ** (-np.arange(n_freq, dtype=np.float32) / np.float32(n_freq))).astype(np.float32)
    ang = np.concatenate([row_id[:, None] * inv_freq, col_id[:, None] * inv_freq], axis=-1).astype(np.float32)
    cos = np.cos(ang).astype(np.float32)
    sin = np.sin(ang).astype(np.float32)
    t = np.zeros((128, 2, TS), np.float32)
    t[:, 0, :] = np.repeat(cos.T, 2, axis=0)
    t[:, 1, :] = np.repeat(sin.T, 2, axis=0)
    return t


def _pack_pp(inp):
    pp = np.zeros((DEPTH, 128, PPW), np.float32)
    for l in range(DEPTH):
        pp[l, :, PP_BMOD:PP_BMOD + 48] = inp["b_mod"][l].reshape(48, 128).T
        for off, nm in ((PP_GPRE1, "g_pre1"), (PP_GPOST1, "g_post1"), (PP_GPRE2, "g_pre2"), (PP_GPOST2, "g_post2")):
            pp[l, :, off:off + 8] = inp[nm][l].reshape(8, 128).T
        pp[l, :, PP_CONVW:PP_CONVW + 12] = inp["conv_w"][l].reshape(3, 4, 128).transpose(2, 1, 0).reshape(128, 12)
        pp[l, :, PP_DNCONVW:PP_DNCONVW + 36] = inp["dn_conv_w"][l].reshape(3, 12, 128).transpose(2, 1, 0).reshape(128, 36)
        pp[l, :, PP_GQN] = inp["g_qn"][l]
        pp[l, :, PP_GKN] = inp["g_kn"][l]
        pp[l, :, PP_DNG] = inp["dn_norm_g"][l]
        pp[l, :, PP_ALOG:PP_ALOG + 8] = inp["dn_a_log"][l].reshape(1, 8)
        pp[l, :, PP_DTB:PP_DTB + 8] = inp["dn_dt_bias"][l].reshape(1, 8)
    return pp


WEIGHTS = ("w_mod", "w_in", "w_pa", "w_pb", "w_pc", "w_o", "w_gate", "w_up", "w_down")


def make_in_maps(inp, cores):
    pp = _pack_pp(inp)
    consts = _consts()
    rope = _rope_tables()
    maps = []
    for c in cores:
        cvec = np.zeros((128, 8, 2), np.float32)
        cvec[:, :, 0] = inp["c"][c].reshape(8, 128).T
        cvec[:, :, 1] = inp["c_ctx"].reshape(8, 128).T
        m = {
            "xs": np.ascontiguousarray(inp["x_sample"][c]),
            "xp": np.ascontiguousarray(inp["x_prompt"][4 * c:4 * c + 4].reshape(TP, D)),
            "cvec": cvec, "pp": pp, "consts": consts, "rope": rope,
            "ck": np.ascontiguousarray(inp["cache_k"][c].reshape(DEPTH, PAST, 256)),
            "cvv": np.ascontiguousarray(inp["cache_v"][c].reshape(DEPTH, PAST, 256)),
            "sdn": np.ascontiguousarray(inp["state_dn"][c]),
        }
        for w in WEIGHTS:
            m[w] = inp[w]
        maps.append(m)
    return maps


def kernel(**inputs):
    inp = {k: np.asarray(v) for k, v in inputs.items()}
    nc = build_program()
    maps = make_in_maps(inp, range(8))
    res = run_bass_kernel_spmd(nc, maps, core_ids=list(range(8)))
    r = res.results
    y_s = np.stack([r[c]["y_s"] for c in range(8)], 0)
    y_p = np.concatenate([r[c]["y_p"].reshape(4, 256, D) for c in range(8)], 0)
    nk = np.concatenate([r[c]["nk"].reshape(4, DEPTH, 256, 2, 128) for c in range(8)], 0)
    nv = np.concatenate([r[c]["nv"].reshape(4, DEPTH, 256, 2, 128) for c in range(8)], 0)
    ns = np.concatenate([r[c]["ns"] for c in range(8)], 0)
    return (y_p, y_s, nk, nv, ns)
```

```python
import numpy as np
from contextlib import ExitStack
import concourse.bass as bass
import concourse.mybir as mybir
from concourse.bass_utils import run_bass_kernel_spmd

F32 = mybir.dt.float32
BF16 = mybir.dt.bfloat16
AF = mybir.ActivationFunctionType
ALU = mybir.AluOpType
AX = mybir.AxisListType

D = 1024
DEPTH = 4
TS = 4096
TP = 1024
NT = TS + TP
TB = 512
NB = NT // TB
D_IN = 8208
D_FF = 2816
EPS = 1e-6
PAST = 512


class Buf:
    __slots__ = ("name", "w", "r", "multi", "excl")

    def __init__(self, name, multi=False, excl=False):
        self.name = name
        self.w = {}
        self.r = {}
        self.multi = multi
        self.excl = excl


class Sched:
    def __init__(self, nc, es):
        self.nc = nc
        self.eng = {"pe": nc.tensor, "act": nc.scalar, "dve": nc.vector,
                    "pool": nc.gpsimd, "sp": nc.sync}
        self.csem = {e: es.enter_context(nc.semaphore("c_" + e)) for e in ("pe", "act", "dve", "pool")}
        self.cnt = {e: 0 for e in self.csem}
        self.known = {e: {} for e in self.eng}
        self.dsem = {}
        self.dval = {}
        self.dnext = {}
        for q, n in (("sp", 24), ("pool", 16), ("act", 4)):
            self.dsem[q] = [es.enter_context(nc.semaphore(f"d_{q}{i}")) for i in range(n)]
            self.dval[q] = [0] * n
            self.dnext[q] = 0
        self.nwait = 0
        self.nins = 0

    def _need(self, e, sem, val, out):
        if val <= 0:
            return
        k = self.known[e]
        if k.get(sem, 0) >= val:
            return
        if e == "pe" and sem is self.csem["pe"]:
            return
        k[sem] = val
        for i, (s2, v2) in enumerate(out):
            if s2 is sem:
                out[i] = (sem, max(val, v2))
                return
        out.append((sem, val))

    def _wait(self, e, sem, val):
        out = []
        self._need(e, sem, val, out)
        for s2, v2 in out:
            self.eng[e].wait_ge(s2, v2)
            self.nwait += 1

    def _deps(self, e, reads, writes):
        out = []
        own = self.csem.get(e)
        for b in reads:
            for sem, val in b.w.items():
                self._need(e, sem, val, out)
            if b.excl:
                for sem, val in b.r.items():
                    if sem is not own:
                        self._need(e, sem, val, out)
        for b in writes:
            if not b.multi:
                for sem, val in b.w.items():
                    self._need(e, sem, val, out)
            for sem, val in b.r.items():
                self._need(e, sem, val, out)
        return out

    def _emit_waits(self, e, waits, ins_fn):
        for s2, v2 in waits[:-1]:
            self.eng[e].wait_ge(s2, v2)
            self.nwait += 1
        ins = ins_fn()
        if waits:
            ins._wait_ge(*waits[-1])
        return ins

    def _record(self, sem, val, reads, writes):
        for b in reads:
            if b.r.get(sem, 0) < val:
                b.r[sem] = val
        for b in writes:
            if b.multi:
                if b.w.get(sem, 0) < val:
                    b.w[sem] = val
            else:
                b.w = {sem: val}
            b.r = {}

    def op(self, e, fn, reads=(), writes=(), inc=True):
        waits = self._deps(e, reads, writes)
        ins = self._emit_waits(e, waits, lambda: fn(self.eng[e]))
        self.nins += 1
        sem = self.csem[e]
        if inc:
            self.cnt[e] += 1
            ins.then_inc(sem, 1)
            val = self.cnt[e]
        else:
            val = self.cnt[e] + 1
        self._record(sem, val, reads, writes)
        return ins

    def dma(self, q, out, in_, reads=(), writes=(), **kw):
        waits = self._deps(q, reads, writes)
        i = self.dnext[q]
        self.dnext[q] = (i + 1) % len(self.dsem[q])
        sem = self.dsem[q][i]
        self._need(q, sem, self.dval[q][i], waits)
        self.dval[q][i] += 16
        ins = self._emit_waits(q, waits, lambda: self.eng[q].dma_start(out=out, in_=in_, **kw))
        ins.then_inc(sem, 16)
        self.nins += 1
        self._record(sem, self.dval[q][i], reads, writes)

    def all_tokens(self):
        toks = [(self.csem[e], self.cnt[e]) for e in self.csem]
        for q in self.dsem:
            toks += list(zip(self.dsem[q], self.dval[q]))
        return toks

    def barrier(self):
        toks = self.all_tokens()
        for e in self.eng:
            for sem, val in toks:
                self._wait(e, sem, val)

    def finish(self):
        for sem, val in self.all_tokens():
            self._wait("sp", sem, val)


C_CB, C_CC, C_CX = 0, 512, 1024
C_AQ, C_AK, C_AV = 1536, 2560, 2816
C_DQ, C_DK, C_DV, C_DZ = 3072, 3584, 4096, 4608
C_BETA, C_A = 5120, 5128
C_GATE = 5136
PP_BMOD = 0
PP_GPRE1 = 48
PP_GPOST1 = 56
PP_GPRE2 = 64
PP_GPOST2 = 72
PP_CONVW = 80
PP_DNCONVW = 92
PP_GQN = 128
PP_GKN = 129
PP_DNG = 130
PP_ALOG = 131
PP_DTB = 139
PPW = 148


NCONST = 10


def build_program(depth=DEPTH, stop_after=None, debug_out=(), skip=()):
    nc = bass.Bass("TRN2", target_bir_lowering=False)
    dt = nc.dram_tensor

    def din(name, shape, dtype=F32):
        return dt(name, list(shape), dtype, kind="ExternalInput").ap()

    def dout(name, shape, dtype=F32):
        return dt(name, list(shape), dtype, kind="ExternalOutput").ap()

    def dscr(name, shape, dtype=F32):
        kind = "ExternalOutput" if name in debug_out else "Internal"
        return dt(name, list(shape), dtype, kind=kind).ap()

    xs_in = din("xs", [TS, D])
    xp_in = din("xp", [TP, D])
    cvec_in = din("cvec", [128, 8, 2])
    pp_in = din("pp", [DEPTH, 128, PPW])
    consts_in = din("consts", [128, NCONST, 128])
    rope_in = din("rope", [128, 2, TS])
    ck_in = din("ck", [DEPTH, PAST, 256])
    cvv_in = din("cvv", [DEPTH, PAST, 256])
    sdn_in = din("sdn", [DEPTH, 2, 4, 128, 128])
    w_mod = din("w_mod", [DEPTH, D, 6 * D])
    w_in = din("w_in", [DEPTH, D, D_IN])
    w_pa = din("w_pa", [DEPTH, 512, D])
    w_pb = din("w_pb", [DEPTH, 1024, D])
    w_pc = din("w_pc", [DEPTH, 512, D])
    w_o = din("w_o", [DEPTH, D, D])
    w_gate = din("w_gate", [DEPTH, D, D_FF])
    w_up = din("w_up", [DEPTH, D, D_FF])
    w_down = din("w_down", [DEPTH, D_FF, D])

    xT = dscr("xT", [D, NT])
    projT = dscr("projT", [D_IN, NT])
    brT = dscr("brT", [2048, NT], BF16)
    actT = dscr("actT", [D_FF, NT], BF16)
    y_s = dout("y_s", [TS, D])
    y_p = dout("y_p", [TP, D])
    nk_out = dout("nk", [4, DEPTH, 256, 256])
    nv_out = dout("nv", [4, DEPTH, 256, 256])
    ns_out = dout("ns", [4, DEPTH, 2, 4, 128, 128])

    xT_v = xT.rearrange("(kc p) t -> p kc t", p=128)

    with ExitStack() as es:
        S = Sched(nc, es)

        uid = {"n": 0}

        def sb(name, shape, dtype=F32, ctx=es):
            uid["n"] += 1
            return ctx.enter_context(nc.sbuf_tensor(f"sb{uid['n']}_{name}", list(shape), dtype))

        psum = [es.enter_context(nc.psum_tensor(f"ps{i}", [128, 512], F32)) for i in range(8)]
        psb = [Buf(f"ps{i}", excl=True) for i in range(8)]

        cst = sb("cst", [128, NCONST, 128])
        b_cst = Buf("cst")
        S.dma("sp", cst[:], consts_in[:, :, :], writes=[b_cst])
        ident = cst[:, 0, :]
        ones = cst[:, 1, :]
        pswapT = cst[:, 2, :]
        onesb = sb("onesb", [128, 128], BF16)
        S.op("dve", lambda e: e.tensor_copy(onesb[:], ones), reads=[b_cst], writes=[b_cst])

        b_xT = Buf("xT", multi=True)
        b_proj = Buf("projT", multi=True)
        b_br = Buf("brT", multi=True)
        b_actT = Buf("actT", multi=True)
        b_out = Buf("outs", multi=True)

        def evac(k, out, in_, reads, writes):
            if k % 2 == 0:
                S.op("act", lambda e: e.copy(out, in_), reads=reads, writes=writes)
            else:
                S.op("dve", lambda e: e.tensor_copy(out, in_), reads=reads, writes=writes)

        with ExitStack() as ph:
            xin = [sb(f"xin{i}", [128, D], ctx=ph) for i in range(2)]
            b_xin = [Buf(f"xin{i}") for i in range(2)]
            xo = [sb(f"xo{i}", [128, 8, 128], ctx=ph) for i in range(2)]
            b_xo = [Buf(f"xo{i}") for i in range(2)]
            for tt in range(NT // 128):
                i = tt % 2
                src = xs_in[tt * 128:(tt + 1) * 128, :] if tt < TS // 128 else \
                    xp_in[(tt - TS // 128) * 128:(tt - TS // 128 + 1) * 128, :]
                S.dma("sp", xin[i][:], src, writes=[b_xin[i]])
                for half in range(2):
                    pb = (tt * 2 + half) % 8
                    for j in range(4):
                        kc = half * 4 + j
                        S.op("pe", lambda e, kc=kc, j=j, pb=pb, i=i: e.transpose(
                            psum[pb][:, j * 128:(j + 1) * 128], xin[i][:, kc * 128:(kc + 1) * 128], ident),
                            reads=[b_xin[i], b_cst], writes=[psb[pb]], inc=(j == 3))
                    evac(half, xo[i][:, half * 4:(half + 1) * 4, :].rearrange("p a b -> p (a b)"), psum[pb][:],
                         [psb[pb]], [b_xo[i]])
                S.dma("pool", xT_v[:, :, tt * 128:(tt + 1) * 128], xo[i][:], reads=[b_xo[i]], writes=[b_xT])
            S.barrier()

        b_hT = Buf("hT", multi=True)
        b_hTd = Buf("hT_d", multi=True)
        pp = sb("pp", [128, PPW])
        b_pp = Buf("pp")
        cv = sb("cvec", [128, 8, 2])
        b_cv = Buf("cvec")
        scv = sb("scvec", [128, 8, 2])
        b_scv = Buf("scvec")
        modT = sb("modT", [128, 48, 2])
        b_mod = Buf("modT")
        modc = sb("modc", [128, 6, 8, 2])
        b_modc = Buf("modc")
        sq = [sb(f"sq{i}", [128, TB]) for i in range(2)]
        b_sq = [Buf(f"sq{i}") for i in range(2)]
        rstd = sb("rstd", [128, TB])
        b_rstd = Buf("rstd")
        S.dma("sp", cv[:], cvec_in[:, :, :], writes=[b_cv])
        S.op("act", lambda e: e.activation(out=scv[:], in_=cv[:], func=AF.Silu), reads=[b_cv], writes=[b_scv])

        PS_NORM = 7

        sqi = {"n": 0}

        def rms_rstd(src_fn, b_src, nch, n, inv_n):
            for kc in range(nch):
                qi = sqi["n"] % 2
                sqi["n"] += 1
                S.op("act", lambda e: e.activation(out=sq[qi][:, 0:n], in_=src_fn(kc), func=AF.Square),
                     reads=[b_src], writes=[b_sq[qi]])
                S.op("pe", lambda e: e.matmul(psum[PS_NORM][:, 0:n], ones, sq[qi][:, 0:n],
                                              start=(kc == 0), stop=(kc == nch - 1)),
                     reads=[b_sq[qi], b_cst], writes=[psb[PS_NORM]])
            S.op("dve", lambda e: e.tensor_scalar(rstd[:, 0:n], psum[PS_NORM][:, 0:n], inv_n, EPS, ALU.mult, ALU.add),
                 reads=[psb[PS_NORM]], writes=[b_rstd])
            S.op("act", lambda e: e.activation(out=rstd[:, 0:n], in_=rstd[:, 0:n], func=AF.Ln),
                 reads=[b_rstd], writes=[b_rstd])
            S.op("act", lambda e: e.activation(out=rstd[:, 0:n], in_=rstd[:, 0:n], func=AF.Exp, scale=-0.5),
                 reads=[b_rstd], writes=[b_rstd])

        tmpn = [sb(f"tmpn{i}", [128, TB]) for i in range(2)]
        b_tmpn = [Buf(f"tmpn{i}") for i in range(2)]

        def norm_mod_block(xb_t, b_x, tb, which, dst_fn, b_dst):
            g = 0 if tb < TS // TB else 1
            rms_rstd(lambda kc: xb_t[:, kc, :], b_x, 8, TB, 1.0 / D)
            for kc in range(8):
                j = kc % 2
                S.op("dve", lambda e: e.scalar_tensor_tensor(
                    tmpn[j][:], xb_t[:, kc, :], modc[:, which * 3 + 0, kc, g:g + 1], rstd[:], ALU.mult, ALU.mult),
                    reads=[b_x, b_modc, b_rstd], writes=[b_tmpn[j]])
                S.op("act", lambda e: e.activation(
                    out=dst_fn(kc), in_=tmpn[j][:], func=AF.Identity,
                    bias=modc[:, which * 3 + 1, kc, g:g + 1], scale=1.0),
                    reads=[b_tmpn[j], b_modc], writes=[b_dst])

        def load_weight_bf16(dst_bf, b_dst, src_ap, stage_list, b_stage_list, k, shape_sl):
            i = k % len(stage_list)
            S.dma("sp", shape_sl(stage_list[i]), src_ap, writes=[b_stage_list[i]])
            S.op("pool", lambda e: e.tensor_copy(dst_bf, shape_sl(stage_list[i])),
                 reads=[b_stage_list[i]], writes=[b_dst])

        bgt = sb("bgt", [128, 40, 16])
        b_bgt = Buf("bgt", multi=True)
        hT_d = dscr("hT_d", [D, NT], BF16)
        hTd_v = hT_d.rearrange("(kc p) t -> p kc t", p=128)
        dn_o = dscr("dn_o", [2, 512, NT])
        gT = dscr("gT", [3072, NT], BF16)
        b_gT = Buf("gT", multi=True)
        b_dno = Buf("dn_o", multi=True)
        seqs = [(0, 32)] + [(32 + 2 * i, 2) for i in range(4)]
        pieces = [(i * 1024, 1024, i == 0, i == 3) for i in range(4)] + \
                 [(TS + i * 256, 256, True, True) for i in range(4)]

        def conv3(dst, b_dst, src, b_s, n, wcol):
            S.op("dve", lambda e: e.tensor_scalar(dst[:, 0:n], src[:, 0:n], pp[:, wcol:wcol + 1], None, ALU.mult),
                 reads=[b_s, b_pp], writes=[b_dst])
            for k in (1, 2):
                S.op("dve", lambda e: e.scalar_tensor_tensor(
                    dst[:, 0:n], src[:, k:k + n], pp[:, wcol + k:wcol + k + 1], dst[:, 0:n], ALU.mult, ALU.add),
                    reads=[b_s, b_pp, b_dst], writes=[b_dst])

        def load_halo(dst, b_dst, row0, t0, n, s0, s1):
            lo = t0 if s0 else t0 - 1
            hi = t0 + n if s1 else t0 + n + 1
            if s0:
                S.op("pool", lambda e: e.memset(dst[:, 0:1], 0.0), writes=[b_dst])
            if s1:
                S.op("pool", lambda e: e.memset(dst[:, n + 1:n + 2], 0.0), writes=[b_dst])
            off = 1 if s0 else 0
            S.dma("sp", dst[:, off:off + (hi - lo)], projT[row0:row0 + 128, lo:hi], reads=[b_proj], writes=[b_dst])

        for l in range(depth):
            S.dma("sp", pp[:], pp_in[l, :, :], writes=[b_pp])
            with ExitStack() as ph:
                wm = [sb(f"wm{i}", [128, 8, 768], ctx=ph) for i in range(2)]
                b_wm = [Buf(f"wm{i}") for i in range(2)]
                for pc in range(8):
                    i = pc % 2
                    S.dma("sp", wm[i][:], w_mod[l, :, pc * 768:(pc + 1) * 768].rearrange("(kc p) m -> p kc m", p=128),
                          writes=[b_wm[i]])
                    for o in range(6):
                        oc = pc * 6 + o
                        for kc in range(8):
                            S.op("pe", lambda e: e.matmul(
                                psum[0][:, oc * 2:oc * 2 + 2], wm[i][:, kc, o * 128:(o + 1) * 128], scv[:, kc, :],
                                start=(kc == 0), stop=(kc == 7)),
                                reads=[b_wm[i], b_scv], writes=[psb[0]], inc=(kc == 7))
                for g in range(2):
                    S.op("dve", lambda e: e.tensor_tensor(
                        modT[:, :, g], psum[0][:, g:96:2], pp[:, PP_BMOD:PP_BMOD + 48], ALU.add),
                        reads=[psb[0], b_pp], writes=[b_mod])
                for g in range(2):
                    for which, (sh_o, sc_o, gt_o, gpre, gpost) in enumerate(
                            ((0, 8, 16, PP_GPRE1, PP_GPOST1), (24, 32, 40, PP_GPRE2, PP_GPOST2))):
                        S.op("dve", lambda e: e.scalar_tensor_tensor(
                            modc[:, which * 3 + 0, :, g], modT[:, sc_o:sc_o + 8, g], 1.0, pp[:, gpre:gpre + 8],
                            ALU.add, ALU.mult), reads=[b_mod, b_pp], writes=[b_modc])
                        S.op("dve", lambda e: e.tensor_copy(
                            modc[:, which * 3 + 1, :, g], modT[:, sh_o:sh_o + 8, g]),
                            reads=[b_mod], writes=[b_modc])
                        S.op("dve", lambda e: e.tensor_tensor(
                            modc[:, which * 3 + 2, :, g], modT[:, gt_o:gt_o + 8, g], pp[:, gpost:gpost + 8],
                            ALU.mult), reads=[b_mod, b_pp], writes=[b_modc])
                S.barrier()

            lay = ExitStack()
            vsb = sb("vsb", [128, 44, 256], BF16, ctx=lay)
            b_vsb = Buf("vsb", multi=True)
            with ExitStack() as hsc:
                hT = sb("hT", [128, 8, NT], BF16, ctx=hsc)
                with ExitStack() as ph:
                    xb = [sb(f"xb{i}", [128, 8, TB], ctx=ph) for i in range(2)]
                    b_xb = [Buf(f"xb{i}") for i in range(2)]
                    for tb in range(NB):
                        i = tb % 2
                        S.dma("sp", xb[i][:], xT_v[:, :, tb * TB:(tb + 1) * TB], reads=[b_xT], writes=[b_xb[i]])
                        norm_mod_block(xb[i], b_xb[i], tb, 0, lambda kc: hT[:, kc, tb * TB:(tb + 1) * TB], b_hT)
                    S.barrier()

                with ExitStack() as ph:
                    wf = [sb(f"wf{i}", [128, 8, 256], ctx=ph) for i in range(2)]
                    b_wf = [Buf(f"wf{i}") for i in range(2)]
                    wb = [sb(f"wb{i}", [128, 8, 256], BF16, ctx=ph) for i in range(2)]
                    b_wb = [Buf(f"wb{i}") for i in range(2)]
                    stg = [sb(f"stg{i}", [128, TB], ctx=ph) for i in range(4)]
                    b_stg = [Buf(f"stg{i}") for i in range(4)]
                    stgb = [sb(f"stgb{i}", [128, TB], BF16, ctx=ph) for i in range(4)]
                    b_stgb = [Buf(f"stgb{i}") for i in range(4)]

                    def load_w(i, c0, M):
                        S.dma("sp", wf[i][:, :, 0:M], w_in[l, :, c0:c0 + M].rearrange("(kc p) m -> p kc m", p=128),
                              writes=[b_wf[i]])
                        S.op("pool", lambda e: e.tensor_copy(wb[i][:, :, 0:M], wf[i][:, :, 0:M]),
                             reads=[b_wf[i]], writes=[b_wb[i]])

                    load_w(0, C_AV, 256)
                    load_w(1, C_BETA, 16)
                    for a2 in range(2):
                        S.dma("sp", stg[a2][:].rearrange("p (a b) -> p a b", a=2),
                              cvv_in[l, a2 * 256:(a2 + 1) * 256, :].rearrange("(a p) m -> p a m", p=128), writes=[b_stg[a2]])
                        S.op("dve", lambda e: e.tensor_copy(vsb[:, 2 * a2:2 * a2 + 2, :].rearrange("p a b -> p (a b)"), stg[a2][:]),
                             reads=[b_stg[a2]], writes=[b_vsb])
                    for tt in range(0 if stop_after == 'P1' else NT // 128):
                        pb = tt % 4
                        si = tt % 4
                        for kc in range(8):
                            S.op("pe", lambda e: e.matmul(
                                psum[pb][:, 0:256], hT[:, kc, tt * 128:(tt + 1) * 128], wb[0][:, kc, :],
                                start=(kc == 0), stop=(kc == 7)),
                                reads=[b_wb[0], b_hT], writes=[psb[pb]], inc=(kc == 7))
                        evac(tt, vsb[:, 4 + tt, :], psum[pb][:, 0:256], [psb[pb]], [b_vsb])
                        if tt >= TS // 128 and "nv" not in skip:
                            p0 = (tt - TS // 128) * 128
                            S.op("dve", lambda e: e.tensor_copy(stg[si][:, 0:256], psum[pb][:, 0:256]),
                                 reads=[psb[pb]], writes=[b_stg[si]])
                            S.dma("sp", (y_p[p0:p0 + 128, 0:256] if "nvtest" in skip else nv_out[p0 // 256, l, p0 % 256:p0 % 256 + 128, :]), stg[si][:, 0:256],
                                  reads=[b_stg[si]], writes=[b_out])
                        pb2 = 4 + tt % 2
                        for kc in range(0 if "bgt" in skip else 8):
                            S.op("pe", lambda e: e.matmul(
                                psum[pb2][:, 0:16], hT[:, kc, tt * 128:(tt + 1) * 128], wb[1][:, kc, 0:16],
                                start=(kc == 0), stop=(kc == 7)),
                                reads=[b_wb[1], b_hT], writes=[psb[pb2]], inc=(kc == 7))
                        if "bgt" not in skip:
                            evac(tt + 1, bgt[:, tt, :], psum[pb2][:, 0:16], [psb[pb2]], [b_bgt])
                    chunks = [(c * 128, 128) for c in range(22)] + [(c * 128, 128) for c in range(24, 40)] + \
                             [(C_GATE + c * 128, 128) for c in range(24)]
                    if stop_after in ('P1', 'P2v'):
                        chunks = chunks[:1]
                    it = 0
                    load_w(0, chunks[0][0], 128)
                    for ci, (c0, M) in enumerate(chunks):
                        i = ci % 2
                        if ci + 1 < len(chunks):
                            load_w((ci + 1) % 2, chunks[ci + 1][0], 128)
                        is_gate = c0 >= C_GATE
                        for tb in range(NB):
                            pb = 4 + (it % 3)
                            si = it % 4
                            for kc in range(8):
                                S.op("pe", lambda e: e.matmul(
                                    psum[pb][0:M, :], wb[i][:, kc, 0:M], hT[:, kc, tb * TB:(tb + 1) * TB],
                                    start=(kc == 0), stop=(kc == 7)),
                                    reads=[b_wb[i], b_hT], writes=[psb[pb]], inc=(kc == 7))
                            if is_gate:
                                S.op("act", lambda e: e.activation(
                                    out=stgb[si][0:M, :], in_=psum[pb][0:M, :], func=AF.Sigmoid),
                                    reads=[psb[pb]], writes=[b_stgb[si]])
                                S.dma("pool", gT[c0 - C_GATE:c0 - C_GATE + M, tb * TB:(tb + 1) * TB], stgb[si][0:M, :],
                                      reads=[b_stgb[si]], writes=[b_gT])
                            else:
                                evac(it, stg[si][0:M, :], psum[pb][0:M, :], [psb[pb]], [b_stg[si]])
                                S.dma("pool", projT[c0:c0 + M, tb * TB:(tb + 1) * TB], stg[si][0:M, :],
                                      reads=[b_stg[si]], writes=[b_proj])
                            it += 1
                    S.barrier()
            if stop_after in ("P1", "P2v", "P2"):
                lay.close()
                break

            if "sconv" not in skip:
                with ExitStack() as ph:
                    cbuf = {nm: [sb(f"c_{nm}{i}", [128, 1026], ctx=ph) for i in range(2)] for nm in ("cb", "cc", "cx", "acc")}
                    b_cbuf = {nm: [Buf(f"c_{nm}{i}") for i in range(2)] for nm in cbuf}
                    cob = [sb(f"c_o{i}", [128, 1024], BF16, ctx=ph) for i in range(2)]
                    b_cob = [Buf(f"c_o{i}") for i in range(2)]
                    it = 0
                    for fc in range(4):
                        for (t0, n, s0, s1) in pieces:
                            i = it % 2
                            it += 1
                            S.dma("sp", cbuf["cb"][i][:, 0:n], projT[C_CB + fc * 128:C_CB + (fc + 1) * 128, t0:t0 + n],
                                  reads=[b_proj], writes=[b_cbuf["cb"][i]])
                            load_halo(cbuf["cc"][i], b_cbuf["cc"][i], C_CC + fc * 128, t0, n, s0, s1)
                            load_halo(cbuf["cx"][i], b_cbuf["cx"][i], C_CX + fc * 128, t0, n, s0, s1)
                            S.op("pool", lambda e: e.tensor_tensor(
                                cbuf["cc"][i][:, 0:n + 2], cbuf["cc"][i][:, 0:n + 2], cbuf["cx"][i][:, 0:n + 2], ALU.mult),
                                reads=[b_cbuf["cc"][i], b_cbuf["cx"][i]], writes=[b_cbuf["cc"][i]])
                            conv3(cbuf["acc"][i], b_cbuf["acc"][i], cbuf["cc"][i], b_cbuf["cc"][i], n, PP_CONVW + fc * 3)
                            S.op("pool", lambda e: e.tensor_tensor(
                                cob[i][:, 0:n], cbuf["acc"][i][:, 0:n], cbuf["cb"][i][:, 0:n], ALU.mult),
                                reads=[b_cbuf["acc"][i], b_cbuf["cb"][i]], writes=[b_cob[i]])
                            S.dma("pool", brT[fc * 128:(fc + 1) * 128, t0:t0 + n], cob[i][:, 0:n],
                                  reads=[b_cob[i]], writes=[b_br])
                    S.barrier()
            if stop_after == "P3a":
                lay.close()
                break

            if "attn" not in skip:
                with ExitStack() as ph:
                    kT = sb("kT", [128, 2, PAST + NT], BF16, ctx=ph)
                    b_kT = Buf("kT", multi=True)
                    ropet = [sb(f"a_rope{i}", [128, 2, TB], ctx=ph) for i in range(2)]
                    b_rope = [Buf(f"a_rope{i}") for i in range(2)]
                    raw = [sb(f"a_raw{i}", [128, TB], ctx=ph) for i in range(2)]
                    b_raw = [Buf(f"a_raw{i}") for i in range(2)]
                    kn = [sb(f"a_kn{i}", [128, TB], ctx=ph) for i in range(2)]
                    b_kn = [Buf(f"a_kn{i}") for i in range(2)]
                    t1 = [sb(f"a_t1{i}", [128, TB], ctx=ph) for i in range(2)]
                    b_t1 = [Buf(f"a_t1{i}") for i in range(2)]
                    t2 = [sb(f"a_t2{i}", [128, TB], ctx=ph) for i in range(2)]
                    b_t2 = [Buf(f"a_t2{i}") for i in range(2)]
                    qb = [sb(f"a_qb{i}", [128, TB], BF16, ctx=ph) for i in range(2)]
                    b_qb = [Buf(f"a_qb{i}") for i in range(2)]
                    pt = [sb(f"a_pt{i}", [128, TB], BF16, ctx=ph) for i in range(5)]
                    b_pt = [Buf(f"a_pt{i}") for i in range(5)]
                    accA = [sb(f"a_accA{i}", [128, TB], ctx=ph) for i in range(2)]
                    b_accA = [Buf(f"a_accA{i}") for i in range(2)]
                    accB = [sb(f"a_accB{i}", [128, TB], ctx=ph) for i in range(2)]
                    b_accB = [Buf(f"a_accB{i}") for i in range(2)]
                    rec = [sb(f"a_rec{i}", [128, TB], ctx=ph) for i in range(2)]
                    b_rec = [Buf(f"a_rec{i}") for i in range(2)]
                    ob = [sb(f"a_ob{i}", [128, TB], BF16, ctx=ph) for i in range(2)]
                    b_ob = [Buf(f"a_ob{i}") for i in range(2)]
                    ckst = sb("a_ck", [128, 4, 256], ctx=ph)
                    b_ckst = Buf("a_ck")
                    kst = [sb(f"a_kst{i}", [128, 4, 128], ctx=ph) for i in range(2)]
                    b_kst = [Buf(f"a_kst{i}") for i in range(2)]
                    PS_R = 6
                    cnt = {"n": 0}

                    def qk_prep(row0, t0, n, gcol, rope, dst_bf, b_dst):
                        i = cnt["n"] % 2
                        cnt["n"] += 1
                        S.dma("sp", raw[i][:, 0:n], projT[row0:row0 + 128, t0:t0 + n], reads=[b_proj], writes=[b_raw[i]])
                        if rope:
                            S.dma("sp", ropet[i][:, :, 0:n], rope_in[:, :, t0:t0 + n], writes=[b_rope[i]])
                        rms_rstd(lambda kc: raw[i][:, 0:n], b_raw[i], 1, n, 1.0 / 128)
                        S.op("dve", lambda e: e.scalar_tensor_tensor(
                            kn[i][:, 0:n], raw[i][:, 0:n], pp[:, gcol:gcol + 1], rstd[:, 0:n], ALU.mult, ALU.mult),
                            reads=[b_raw[i], b_pp, b_rstd], writes=[b_kn[i]])
                        if rope:
                            S.op("pe", lambda e: e.matmul(psum[PS_R][:, 0:n], pswapT, kn[i][:, 0:n], start=True, stop=True),
                                 reads=[b_kn[i], b_cst], writes=[psb[PS_R]])
                            S.op("pool", lambda e: e.tensor_tensor(t1[i][:, 0:n], kn[i][:, 0:n], ropet[i][:, 0, 0:n], ALU.mult),
                                 reads=[b_kn[i], b_rope[i]], writes=[b_t1[i]])
                            S.op("dve", lambda e: e.tensor_tensor(t2[i][:, 0:n], psum[PS_R][:, 0:n], ropet[i][:, 1, 0:n], ALU.mult),
                                 reads=[psb[PS_R], b_rope[i]], writes=[b_t2[i]])
                            S.op("pool", lambda e: e.tensor_tensor(dst_bf, t1[i][:, 0:n], t2[i][:, 0:n], ALU.add),
                                 reads=[b_t1[i], b_t2[i]], writes=[b_dst])
                        else:
                            S.op("act", lambda e: e.copy(dst_bf, kn[i][:, 0:n]), reads=[b_kn[i]], writes=[b_dst])
                        return kn[i], b_kn[i]

                    S.dma("sp", ckst[:], ck_in[l, :, :].rearrange("(a p) m -> p a m", p=128), writes=[b_ckst])
                    for j in range(2):
                        for a in range(4):
                            S.op("pe", lambda e: e.transpose(
                                psum[j][:, a * 128:(a + 1) * 128], ckst[:, a, j * 128:(j + 1) * 128], ident),
                                reads=[b_ckst, b_cst], writes=[psb[j]], inc=(a == 3))
                        evac(j, kT[:, j, 0:PAST], psum[j][:], [psb[j]], [b_kT])
                    for tb in range(NB):
                        t0 = tb * TB
                        is_s = tb < TS // TB
                        for j in range(2):
                            knt, b_knt = qk_prep(C_AK + j * 128, t0, TB, PP_GKN, is_s,
                                                 kT[:, j, PAST + t0:PAST + t0 + TB], b_kT)
                            if not is_s:
                                si = j
                                pbk = j
                                for a in range(4):
                                    S.op("pe", lambda e: e.transpose(
                                        psum[pbk][:, a * 128:(a + 1) * 128], knt[:, a * 128:(a + 1) * 128], ident),
                                        reads=[b_knt, b_cst], writes=[psb[pbk]], inc=(a == 3))
                                evac(j, kst[si][:].rearrange("p a b -> p (a b)"), psum[pbk][:], [psb[pbk]], [b_kst[si]])
                                for a in range(4):
                                    p0 = t0 - TS + a * 128
                                    S.dma("sp", nk_out[p0 // 256, l, p0 % 256:p0 % 256 + 128, j * 128:(j + 1) * 128],
                                          kst[si][:, a, :], reads=[b_kst[si]], writes=[b_out])
                    segs = [(0, TS, 0, list(range(0, 36)), TB)] + \
                           [(TS + i * 256, 256, PAST + TS + i * 256, [36 + 2 * i, 37 + 2 * i], 256) for i in range(4)]
                    scale = 128.0 ** -0.5
                    tasks = []
                    for (tq0, tqn, kcol0, vchunks, nq) in segs:
                        for h in range(8):
                            for qblk in range(tqn // nq):
                                tasks.append((tq0 == 0, h, tq0 + qblk * nq, nq, kcol0, vchunks))
                    itp = 0

                    def prep(k):
                        is_s, h, t0, nq, kcol0, vchunks = tasks[k]
                        qk_prep(C_AQ + h * 128, t0, nq, PP_GQN, is_s, qb[k % 2][:, 0:nq], b_qb[k % 2])

                    LA = 2
                    STB = (0, 1, 4)
                    PSS = 5
                    prep(0)
                    for k in range(len(tasks)):
                        if k + 1 < len(tasks):
                            prep(k + 1)
                        is_s, h, t0, nq, kcol0, vchunks = tasks[k]
                        j = h // 4
                        qi = k % 2
                        pso = 2 + qi
                        nch = len(vchunks)
                        base = itp
                        itp += nch
                        pe_sum_started = False
                        for step in range(nch + LA):
                            if step < nch:
                                ci = step
                                pst = STB[(base + ci) % 3]
                                pti = (base + ci) % 5
                                kc0 = kcol0 + ci * 128
                                S.op("pe", lambda e: e.matmul(
                                    psum[pst][:, 0:nq], kT[:, j, kc0:kc0 + 128], qb[qi][:, 0:nq], start=True, stop=True),
                                    reads=[b_kT, b_qb[qi]], writes=[psb[pst]])
                                S.op("act", lambda e: e.activation(
                                    out=pt[pti][:, 0:nq], in_=psum[pst][:, 0:nq], func=AF.Exp, scale=scale),
                                    reads=[psb[pst]], writes=[b_pt[pti]])
                                if False:
                                    ae = "dve"
                                    acc_t, b_acc_t = (accA[qi], b_accA[qi])
                                    if ci < 3:
                                        S.op(ae, lambda e: e.tensor_copy(acc_t[:, 0:nq], pt[pti][:, 0:nq]),
                                             reads=[b_pt[pti]], writes=[b_acc_t])
                                    else:
                                        S.op(ae, lambda e: e.tensor_tensor(acc_t[:, 0:nq], acc_t[:, 0:nq], pt[pti][:, 0:nq], ALU.add),
                                             reads=[b_pt[pti], b_acc_t], writes=[b_acc_t])
                            c1 = step - LA
                            if c1 >= 0:
                                pti = (base + c1) % 5
                                vch = vchunks[c1]
                                S.op("pe", lambda e: e.matmul(
                                    psum[pso][:, 0:nq], vsb[:, vch, j * 128:(j + 1) * 128], pt[pti][:, 0:nq],
                                    start=(c1 == 0), stop=(c1 == nch - 1)),
                                    reads=[b_vsb, b_pt[pti]], writes=[psb[pso]])
                                if True:
                                    S.op("pe", lambda e: e.matmul(
                                        psum[PSS][:, 0:nq], onesb[:], pt[pti][:, 0:nq],
                                        start=(not pe_sum_started), stop=(c1 == nch - 1)),
                                        reads=[b_cst, b_pt[pti]], writes=[psb[PSS]])
                                    pe_sum_started = True
                        S.op("dve", lambda e: e.reciprocal(rec[qi][:, 0:nq], psum[PSS][:, 0:nq]),
                             reads=[psb[PSS]], writes=[b_rec[qi]])
                        S.op("dve", lambda e: e.tensor_tensor(
                            ob[qi][:, 0:nq], psum[pso][:, 0:nq], rec[qi][:, 0:nq], ALU.mult),
                            reads=[psb[pso], b_rec[qi]], writes=[b_ob[qi]])
                        S.dma("pool", brT[512 + h * 128:512 + (h + 1) * 128, t0:t0 + nq], ob[qi][:, 0:nq],
                              reads=[b_ob[qi]], writes=[b_br])
                    S.barrier()
            lay.close()
            if stop_after == "P3b":
                break

            if "dn" not in skip:
                with ExitStack() as ph:
                    W8 = [128, 40, 8]
                    nbeta = sb("d_nbeta", W8, ctx=ph)
                    gtk = sb("d_g", W8, ctx=ph)
                    Gc = sb("d_G", W8, ctx=ph)
                    eG = sb("d_eG", W8, ctx=ph)
                    egl = sb("d_egl", W8, ctx=ph)
                    kdsc = sb("d_kdsc", W8, ctx=ph)
                    tA = sb("d_tA", W8, ctx=ph)
                    tB = sb("d_tB", W8, ctx=ph)
                    nega = sb("d_nega", [128, 8], ctx=ph)
                    b_g8 = Buf("d_gate8")
                    S.op("act", lambda e: e.activation(out=nbeta[:], in_=bgt[:, :, 0:8], func=AF.Sigmoid),
                         reads=[b_bgt], writes=[b_g8])
                    S.op("dve", lambda e: e.tensor_scalar(nbeta[:], nbeta[:], -1.0, None, ALU.mult), reads=[b_g8], writes=[b_g8])
                    S.op("act", lambda e: e.activation(out=nega[:], in_=pp[:, PP_ALOG:PP_ALOG + 8], func=AF.Exp),
                         reads=[b_pp], writes=[b_g8])
                    S.op("dve", lambda e: e.tensor_scalar(nega[:], nega[:], -1.0, None, ALU.mult), reads=[b_g8], writes=[b_g8])
                    for c in range(8):
                        S.op("dve", lambda e: e.tensor_scalar(tA[:, :, c], bgt[:, :, 8 + c], pp[:, PP_DTB + c:PP_DTB + c + 1], None, ALU.add),
                             reads=[b_bgt, b_pp, b_g8], writes=[b_g8])
                    S.op("act", lambda e: e.activation(out=tB[:], in_=tA[:], func=AF.Abs), reads=[b_g8], writes=[b_g8])
                    S.op("act", lambda e: e.activation(out=tB[:], in_=tB[:], func=AF.Exp, scale=-1.0), reads=[b_g8], writes=[b_g8])
                    S.op("act", lambda e: e.activation(out=tB[:], in_=tB[:], func=AF.Ln, bias=1.0, scale=1.0), reads=[b_g8], writes=[b_g8])
                    S.op("dve", lambda e: e.tensor_scalar(tA[:], tA[:], 0.0, None, ALU.max), reads=[b_g8], writes=[b_g8])
                    S.op("dve", lambda e: e.tensor_tensor(tA[:], tA[:], tB[:], ALU.add), reads=[b_g8], writes=[b_g8])
                    for c in range(8):
                        S.op("dve", lambda e: e.tensor_scalar(gtk[:, :, c], tA[:, :, c], nega[:, c:c + 1], None, ALU.mult),
                             reads=[b_g8], writes=[b_g8])
                    for tt in range(40):
                        pbk = tt % 2
                        for d in range(2):
                            S.op("pe", lambda e: e.matmul(psum[pbk][:, d * 4:d * 4 + 4], cst[:, 3 + d, :], gtk[:, tt, d * 4:d * 4 + 4],
                                                          start=True, stop=True), reads=[b_cst, b_g8], writes=[psb[pbk]])
                        S.op("pe", lambda e: e.matmul(psum[pbk][:, 8:16], ones, gtk[:, tt, :], start=True, stop=True),
                             reads=[b_cst, b_g8], writes=[psb[pbk]])
                        S.op("dve", lambda e: e.tensor_copy(Gc[:, tt, :], psum[pbk][:, 0:8]), reads=[psb[pbk]], writes=[b_g8])
                        S.op("act", lambda e: e.copy(egl[:, tt, :], psum[pbk][:, 8:16]), reads=[psb[pbk]], writes=[b_g8])
                    S.op("dve", lambda e: e.tensor_tensor(kdsc[:], egl[:], Gc[:], ALU.subtract), reads=[b_g8], writes=[b_g8])
                    S.op("act", lambda e: e.activation(out=kdsc[:], in_=kdsc[:], func=AF.Exp), reads=[b_g8], writes=[b_g8])
                    S.op("act", lambda e: e.activation(out=egl[:], in_=egl[:], func=AF.Exp), reads=[b_g8], writes=[b_g8])
                    S.op("act", lambda e: e.activation(out=eG[:], in_=Gc[:], func=AF.Exp), reads=[b_g8], writes=[b_g8])

                    qh = sb("d_qh", [128, NT], ctx=ph)
                    kh = sb("d_kh", [128, NT], ctx=ph)
                    vh = sb("d_vh", [128, NT], ctx=ph)
                    b_qkv = Buf("d_qkv", multi=True)
                    b_qraw = Buf("d_qraw", multi=True)
                    ktok = sb("d_ktok", [128, 40, 128], ctx=ph)
                    vtok = sb("d_vtok", [128, 40, 128], ctx=ph)
                    b_tok = Buf("d_tok", multi=True)
                    NU = 8
                    NB_PRE = 6
                    psl = {"n": 0, "o": 0}

                    def pslot():
                        i = psl["n"] % 24
                        psl["n"] += 1
                        return psum[i % 6][:, (i // 6) * 128:(i // 6 + 1) * 128], psb[i % 6]

                    evk = {"n": 0}

                    def evac2(out, in_, reads, writes):
                        evk["n"] += 1
                        evac(evk["n"], out, in_, reads, writes)

                    for h in range(4):
                        with ExitStack() as pa:
                            cin = [sb(f"d_cin{i}", [128, 1026], ctx=pa) for i in range(2)]
                            b_cin = [Buf(f"d_cin{i}") for i in range(2)]
                            cacc = [sb(f"d_cacc{i}", [128, 1024], ctx=pa) for i in range(2)]
                            b_cacc = [Buf(f"d_cacc{i}") for i in range(2)]
                            it = 0
                            for (dst, row0, wc, kind) in ((qh, C_DQ + h * 128, h, "q"), (kh, C_DK + h * 128, 4 + h, "k"),
                                                         (vh, C_DV + h * 128, 8 + h, "v")):
                                for (t0, n, s0, s1) in pieces:
                                    i = it % 2
                                    it += 1
                                    load_halo(cin[i], b_cin[i], row0, t0, n, s0, s1)
                                    conv3(cacc[i], b_cacc[i], cin[i], b_cin[i], n, PP_DNCONVW + wc * 3)
                                    S.op("act", lambda e: e.activation(out=dst[:, t0:t0 + n], in_=cacc[i][:, 0:n], func=AF.Silu),
                                         reads=[b_cacc[i]], writes=[b_qkv if kind == "v" else b_qraw])
                            for (dst, sc_) in ((qh, 128.0 ** -0.5), (kh, 1.0)):
                                for tb in range(NB):
                                    blk = dst[:, tb * TB:(tb + 1) * TB]
                                    rms_rstd(lambda kc: blk, b_qraw, 1, TB, 1.0)
                                    S.op("dve", lambda e: e.scalar_tensor_tensor(blk, blk, sc_, rstd[:, 0:TB], ALU.mult, ALU.mult),
                                         reads=[b_qraw, b_rstd], writes=[b_qkv])
                            for (src, dstt) in ((kh, ktok), (vh, vtok)):
                                for t4 in range(10):
                                    pbk = t4 % 2
                                    for a in range(4):
                                        tt = t4 * 4 + a
                                        S.op("pe", lambda e: e.transpose(psum[pbk][:, a * 128:(a + 1) * 128], src[:, tt * 128:(tt + 1) * 128], ident),
                                             reads=[b_qkv, b_cst], writes=[psb[pbk]], inc=(a == 3))
                                    evac2(dstt[:, t4 * 4:(t4 + 1) * 4, :].rearrange("p a b -> p (a b)"), psum[pbk][:], [psb[pbk]], [b_tok])
                            S.barrier()

                        with ExitStack() as pc:
                            ures = {nm: [sb(f"d_{nm}{i}", [128, 128], (BF16 if nm in ("RTb", "R", "x0") else F32), ctx=pc)
                                         for i in range(NU)]
                                    for nm in ("RT", "RTb", "R", "x0", "xT0", "attnT", "qgT", "kdec")}
                            b_ures = {nm: [Buf(f"d_{nm}{i}") for i in range(NU)] for nm in ures}
                            wpool = [[sb(f"d_wk{s_}_{i}", [128, 128], (F32 if i < 4 else BF16), ctx=pc) for i in range(8)]
                                     for s_ in range(NB_PRE)]
                            b_wpool = [[Buf(f"d_wk{s_}_{i}") for i in range(8)] for s_ in range(NB_PRE)]
                            spool = [[sb(f"d_sk{d}_{i}", [128, 128], ctx=pc) for i in range(4)] for d in range(2)]
                            b_spool = [[Buf(f"d_sk{d}_{i}") for i in range(4)] for d in range(2)]
                            Sst = [[sb(f"d_S{c}_{i}", [128, 128], ctx=pc) for i in range(2)] for c in range(2)]
                            b_Sst = [[Buf(f"d_S{c}_{i}") for i in range(2)] for c in range(2)]
                            ostg = [[sb(f"d_ostg{d}_{i}", [128, 128], ctx=pc) for i in range(2)] for d in range(2)]
                            b_ostg = [[Buf(f"d_ostg{d}_{i}") for i in range(2)] for d in range(2)]

                            def precompute(tt, d, u, slot):
                                c = d * 4 + h
                                tsl = slice(tt * 128, (tt + 1) * 128)
                                tri = cst[:, 3 + d, :]
                                wn = {"n": 0}

                                def wtile():
                                    i = wn["n"]
                                    wn["n"] += 1
                                    i = i if i < 4 else 4 + (i - 4) % 4
                                    return wpool[slot][i], b_wpool[slot][i]
                                pn_ = {"n": 0}

                                def pslot():
                                    i = pn_["n"] % 4
                                    pn_["n"] += 1
                                    return psum[slot][:, i * 128:(i + 1) * 128], psb[slot]
                                xT0, b_xT0 = ures["xT0"][u], b_ures["xT0"][u]
                                x0, b_x0 = ures["x0"][u], b_ures["x0"][u]
                                RTf, b_RTf = ures["RT"][u], b_ures["RT"][u]
                                RT, b_RT = ures["RTb"][u], b_ures["RTb"][u]
                                R, b_R = ures["R"][u], b_ures["R"][u]
                                pg, bpg = pslot()
                                S.op("pe", lambda e: e.matmul(pg, gtk[:, tt, c:c + 1].to_broadcast([128, 128]), tri, start=True, stop=True),
                                     reads=[b_g8, b_cst], writes=[bpg])
                                pk, bpk = pslot()
                                S.op("pe", lambda e: e.matmul(pk, kh[:, tsl], kh[:, tsl], start=True, stop=True),
                                     reads=[b_qkv], writes=[bpk])
                                pq, bpq = pslot()
                                S.op("pe", lambda e: e.matmul(pq, kh[:, tsl], qh[:, tsl], start=True, stop=True),
                                     reads=[b_qkv], writes=[bpq])
                                S.op("act", lambda e: e.activation(out=ures["kdec"][u][:], in_=ktok[:, tt, :], func=AF.Identity,
                                                                   scale=kdsc[:, tt, c:c + 1]),
                                     reads=[b_tok, b_g8], writes=[b_ures["kdec"][u]])
                                yield
                                dm, b_dm = wtile()
                                S.op("dve", lambda e: e.tensor_scalar(dm[:], pg, Gc[:, tt, c:c + 1], 0.0, ALU.subtract, ALU.min),
                                     reads=[bpg, b_g8], writes=[b_dm])
                                egb, b_egb = wtile()
                                S.op("act", lambda e: e.activation(out=egb[:], in_=pg, func=AF.Exp), reads=[bpg], writes=[b_egb])
                                yield
                                S.op("act", lambda e: e.activation(out=dm[:], in_=dm[:], func=AF.Exp), reads=[b_dm], writes=[b_dm])
                                S.op("pool", lambda e: e.tensor_tensor(ures["qgT"][u][:], qh[:, tsl], egb[:], ALU.mult),
                                     reads=[b_qkv, b_egb], writes=[b_ures["qgT"][u]])
                                yield
                                ei, b_ei = wtile()
                                S.op("pool", lambda e: e.tensor_tensor(ei[:], dm[:], tri, ALU.mult), reads=[b_dm, b_cst], writes=[b_ei])
                                yield
                                es_, b_es = wtile()
                                S.op("pool", lambda e: e.tensor_tensor(es_[:], ei[:], ident, ALU.subtract), reads=[b_ei, b_cst], writes=[b_es])
                                S.op("dve", lambda e: e.tensor_tensor(ures["attnT"][u][:], pq, ei[:], ALU.mult),
                                     reads=[bpq, b_ei], writes=[b_ures["attnT"][u]])
                                yield
                                S.op("dve", lambda e: e.scalar_tensor_tensor(xT0[:], pk, nbeta[:, tt, c:c + 1], es_[:], ALU.mult, ALU.mult),
                                     reads=[bpk, b_g8, b_es], writes=[b_xT0])
                                yield
                                px, bpx = pslot()
                                S.op("pe", lambda e: e.transpose(px, xT0[:], ident), reads=[b_xT0, b_cst], writes=[bpx])
                                XT, b_XT = wtile()
                                S.op("pool", lambda e: e.tensor_tensor(XT[:], xT0[:], cst[:, 5, :], ALU.mult), reads=[b_xT0, b_cst], writes=[b_XT])
                                yield
                                evac2(x0[:], px, [bpx], [b_x0])
                                S.op("pool", lambda e: e.tensor_tensor(RT[:], XT[:], ident, ALU.add), reads=[b_XT, b_cst], writes=[b_RT])
                                yield
                                X, b_X = wtile()
                                S.op("pool", lambda e: e.tensor_tensor(X[:], x0[:], cst[:, 5, :], ALU.mult), reads=[b_x0, b_cst], writes=[b_X])
                                yield
                                S.op("pool", lambda e: e.tensor_tensor(R[:], X[:], ident, ALU.add), reads=[b_X, b_cst], writes=[b_R])
                                for lev in range(3):
                                    pn, bpn = pslot()
                                    S.op("pe", lambda e: e.matmul(pn, XT[:], X[:], start=True, stop=True), reads=[b_XT, b_X], writes=[bpn])
                                    pnt, bpnt = pslot()
                                    S.op("pe", lambda e: e.matmul(pnt, X[:], XT[:], start=True, stop=True), reads=[b_XT, b_X], writes=[bpnt])
                                    yield
                                    Xn, b_Xn = wtile()
                                    evac2(Xn[:], pn, [bpn], [b_Xn])
                                    XTn, b_XTn = wtile()
                                    evac2(XTn[:], pnt, [bpnt], [b_XTn])
                                    yield
                                    pr, bpr = pslot()
                                    S.op("pe", lambda e: e.matmul(pr, Xn[:], RT[:], start=True, stop=True), reads=[b_Xn, b_RT], writes=[bpr])
                                    pr2, bpr2 = pslot()
                                    S.op("pe", lambda e: e.matmul(pr2, XTn[:], R[:], start=True, stop=True), reads=[b_XTn, b_R], writes=[bpr2])
                                    yield
                                    S.op("dve", lambda e: e.tensor_tensor(RT[:], pr, RT[:], ALU.add), reads=[bpr, b_RT], writes=[b_RT])
                                    S.op("dve", lambda e: e.tensor_tensor(R[:], pr2, R[:], ALU.add), reads=[bpr2, b_R], writes=[b_R])
                                    X, b_X, XT, b_XT = Xn, b_Xn, XTn, b_XTn
                                    yield
                                for li in range(3):
                                    last = li == 2
                                    msk = cst[:, 6 + li, :]
                                    YT, b_YT = wtile()
                                    S.op("pool", lambda e: e.tensor_tensor(YT[:], x0[:], msk, ALU.mult), reads=[b_x0, b_cst], writes=[b_YT])
                                    if not last:
                                        Y_, b_Y = wtile()
                                        S.op("pool", lambda e: e.tensor_tensor(Y_[:], xT0[:], msk, ALU.mult), reads=[b_xT0, b_cst], writes=[b_Y])
                                    yield
                                    pp_, bpp_ = pslot()
                                    S.op("pe", lambda e: e.matmul(pp_, YT[:], RT[:], start=True, stop=True), reads=[b_YT, b_RT], writes=[bpp_])
                                    if not last:
                                        pq_, bpq_ = pslot()
                                        S.op("pe", lambda e: e.matmul(pq_, Y_[:], R[:], start=True, stop=True), reads=[b_Y, b_R], writes=[bpq_])
                                    yield
                                    P_, b_P = wtile()
                                    evac2(P_[:], pp_, [bpp_], [b_P])
                                    if not last:
                                        Q_, b_Q = wtile()
                                        evac2(Q_[:], pq_, [bpq_], [b_Q])
                                    yield
                                    pa_, bpa_ = pslot()
                                    S.op("pe", lambda e: e.matmul(pa_, R[:], P_[:], start=True, stop=True), reads=[b_R, b_P], writes=[bpa_])
                                    if not last:
                                        pb_, bpb_ = pslot()
                                        S.op("pe", lambda e: e.matmul(pb_, RT[:], Q_[:], start=True, stop=True), reads=[b_RT, b_Q], writes=[bpb_])
                                    yield
                                    if not last:
                                        S.op("dve", lambda e: e.tensor_tensor(RT[:], pa_, RT[:], ALU.add), reads=[bpa_, b_RT], writes=[b_RT])
                                        S.op("dve", lambda e: e.tensor_tensor(R[:], pb_, R[:], ALU.add), reads=[bpb_, b_R], writes=[b_R])
                                    else:
                                        S.op("dve", lambda e: e.tensor_tensor(RTf[:], pa_, RT[:], ALU.add), reads=[bpa_, b_RT], writes=[b_RTf])
                                    yield

                            units = []
                            for si_, (tile0, ntile) in enumerate(seqs):
                                for s_ in range(ntile):
                                    for d in range(2):
                                        tt = tile0 + s_ if d == 0 else tile0 + ntile - 1 - s_
                                        units.append((tt, d, si_, s_))
                            unit_of = {(si_, s_, d): n for n, (tt, d, si_, s_) in enumerate(units)}
                            pre_done = [False] * len(units)
                            scan_done = [False] * len(units)

                            def chain(d):
                                sn = {"n": 0}

                                def stile():
                                    i = sn["n"] % 4
                                    sn["n"] += 1
                                    return spool[d][i], b_spool[d][i]
                                sp_ = {"n": 0}

                                def pslot():
                                    i = sp_["n"] % 4
                                    sp_["n"] += 1
                                    return psum[6 + d][:, i * 128:(i + 1) * 128], psb[6 + d]
                                c = d * 4 + h
                                for si_, (tile0, ntile) in enumerate(seqs):
                                    if si_ == 0:
                                        S.dma("sp", Sst[d][0][:], sdn_in[l, d, h, :, :], writes=[b_Sst[d][0]])
                                    else:
                                        S.op("pool", lambda e: e.memset(Sst[d][0][:], 0.0), writes=[b_Sst[d][0]])
                                    for s_ in range(ntile):
                                        n = unit_of[(si_, s_, d)]
                                        while not pre_done[n]:
                                            yield
                                        tt = units[n][0]
                                        u = n % NU
                                        tsl = slice(tt * 128, (tt + 1) * 128)
                                        par = s_ % 2
                                        S_old, b_So = Sst[d][par], b_Sst[d][par]
                                        S_new, b_Sn = Sst[d][1 - par], b_Sst[d][1 - par]
                                        p1, bp1 = pslot()
                                        S.op("pe", lambda e: e.matmul(p1, kh[:, tsl], S_old[:], start=True, stop=True), reads=[b_qkv, b_So], writes=[bp1])
                                        yield
                                        nr, b_nr = stile()
                                        S.op("dve", lambda e: e.scalar_tensor_tensor(nr[:], p1, eG[:, tt, c:c + 1], vtok[:, tt, :], ALU.mult, ALU.subtract),
                                             reads=[bp1, b_g8, b_tok], writes=[b_nr])
                                        yield
                                        p2, bp2 = pslot()
                                        S.op("pe", lambda e: e.matmul(p2, ures["RT"][u][:], nr[:], start=True, stop=True),
                                             reads=[b_ures["RT"][u], b_nr], writes=[bp2])
                                        yield
                                        vn, b_vn = stile()
                                        S.op("act", lambda e: e.activation(out=vn[:], in_=p2, func=AF.Identity, scale=nbeta[:, tt, c:c + 1]),
                                             reads=[bp2, b_g8], writes=[b_vn])
                                        yield
                                        po, bpo = pslot()
                                        S.op("pe", lambda e: e.matmul(po, S_old[:], ures["qgT"][u][:], start=True, stop=False),
                                             reads=[b_So, b_ures["qgT"][u]], writes=[bpo], inc=False)
                                        S.op("pe", lambda e: e.matmul(po, vn[:], ures["attnT"][u][:], start=False, stop=True),
                                             reads=[b_vn, b_ures["attnT"][u]], writes=[bpo])
                                        p3, bp3 = pslot()
                                        S.op("pe", lambda e: e.matmul(p3, ures["kdec"][u][:], vn[:], start=True, stop=True),
                                             reads=[b_ures["kdec"][u], b_vn], writes=[bp3])
                                        yield
                                        oi = s_ % 2
                                        evac2(ostg[d][oi][:], po, [bpo], [b_ostg[d][oi]])
                                        S.dma("pool", dn_o[d, h * 128:(h + 1) * 128, tsl], ostg[d][oi][:], reads=[b_ostg[d][oi]], writes=[b_dno])
                                        S.op("dve", lambda e: e.scalar_tensor_tensor(S_new[:], S_old[:], egl[:, tt, c:c + 1], p3, ALU.mult, ALU.add),
                                             reads=[b_So, b_g8, bp3], writes=[b_Sn])
                                        scan_done[n] = True
                                        yield
                                    if si_ > 0:
                                        S.dma("sp", ns_out[si_ - 1, l, d, h, :, :], Sst[d][ntile % 2][:],
                                              reads=[b_Sst[d][ntile % 2]], writes=[b_out])

                            chains = [chain(0), chain(1)]
                            active = [None] * NB_PRE
                            next_pre = 0
                            while chains or any(a is not None for a in active) or next_pre < len(units):
                                for slot in range(NB_PRE):
                                    if active[slot] is None and next_pre < len(units):
                                        n = next_pre
                                        if n < NU or scan_done[n - NU]:
                                            tt, d, si_, s_ = units[n]
                                            active[slot] = (precompute(tt, d, n % NU, slot), n)
                                            next_pre += 1
                                    if active[slot] is not None:
                                        g_, n = active[slot]
                                        try:
                                            next(g_)
                                        except StopIteration:
                                            pre_done[n] = True
                                            active[slot] = None
                                for cg in list(chains):
                                    try:
                                        next(cg)
                                    except StopIteration:
                                        chains.remove(cg)
                            S.barrier()

                    S.barrier()
                with ExitStack() as ph:
                    of_ = [sb(f"e_of{i}", [128, TB], ctx=ph) for i in range(2)]
                    b_of = [Buf(f"e_of{i}") for i in range(2)]
                    ob_ = [sb(f"e_ob{i}", [128, TB], ctx=ph) for i in range(2)]
                    b_ob_ = [Buf(f"e_ob{i}") for i in range(2)]
                    zt = [sb(f"e_z{i}", [128, TB], ctx=ph) for i in range(2)]
                    b_zt = [Buf(f"e_z{i}") for i in range(2)]
                    yo_ = [sb(f"e_y{i}", [128, TB], BF16, ctx=ph) for i in range(2)]
                    b_yo_ = [Buf(f"e_y{i}") for i in range(2)]
                    it = 0
                    for h in range(4):
                        for tb in range(NB):
                            i = it % 2
                            it += 1
                            tsl = slice(tb * TB, (tb + 1) * TB)
                            S.dma("sp", of_[i][:], dn_o[0, h * 128:(h + 1) * 128, tsl], reads=[b_dno], writes=[b_of[i]])
                            S.dma("sp", ob_[i][:], dn_o[1, h * 128:(h + 1) * 128, tsl], reads=[b_dno], writes=[b_ob_[i]])
                            S.dma("sp", zt[i][:], projT[C_DZ + h * 128:C_DZ + (h + 1) * 128, tsl], reads=[b_proj], writes=[b_zt[i]])
                            S.op("pool", lambda e: e.tensor_tensor(of_[i][:], of_[i][:], ob_[i][:], ALU.add),
                                 reads=[b_of[i], b_ob_[i]], writes=[b_of[i]])
                            S.op("act", lambda e: e.activation(out=zt[i][:], in_=zt[i][:], func=AF.Silu), reads=[b_zt[i]], writes=[b_zt[i]])
                            rms_rstd(lambda kc: of_[i][:], b_of[i], 1, TB, 1.0 / 128)
                            S.op("dve", lambda e: e.scalar_tensor_tensor(
                                ob_[i][:], of_[i][:], pp[:, PP_DNG:PP_DNG + 1], rstd[:], ALU.mult, ALU.mult),
                                reads=[b_of[i], b_pp, b_rstd], writes=[b_ob_[i]])
                            S.op("pool", lambda e: e.tensor_tensor(yo_[i][:], ob_[i][:], zt[i][:], ALU.mult),
                                 reads=[b_ob_[i], b_zt[i]], writes=[b_yo_[i]])
                            S.dma("pool", brT[1536 + h * 128:1536 + (h + 1) * 128, tsl], yo_[i][:], reads=[b_yo_[i]], writes=[b_br])
                    S.barrier()
            if stop_after == "P3c":
                break

            with ExitStack() as ph:
                wpa = sb("wpa", [128, 4, D], BF16, ctx=ph)
                wpb = sb("wpb", [128, 8, D], BF16, ctx=ph)
                wpc = sb("wpc", [128, 4, D], BF16, ctx=ph)
                wo = sb("wo", [128, 8, D], BF16, ctx=ph)
                b_w4 = Buf("w4", multi=True)
                wst = [sb(f"wst{i}", [128, 1, D], ctx=ph) for i in range(2)]
                b_wst = [Buf(f"wst{i}") for i in range(2)]
                k = 0
                for (dst, src, nk_) in ((wpa, w_pa, 4), (wpb, w_pb, 8), (wpc, w_pc, 4), (wo, w_o, 8)):
                    for kc in range(nk_):
                        load_weight_bf16(dst[:, kc:kc + 1, :], b_w4,
                                         src[l, kc * 128:(kc + 1) * 128, :].rearrange("(kc p) m -> p kc m", p=128),
                                         wst, b_wst, k, lambda t: t[:])
                        k += 1
                brb = sb("brb", [128, 16, TB], BF16, ctx=ph)
                b_brb = Buf("brb")
                gts = [sb(f"gts{i}", [128, 3, TB], BF16, ctx=ph) for i in range(2)]
                b_gts = [Buf(f"gts{i}") for i in range(2)]
                xb4 = sb("xb4", [128, 8, TB], ctx=ph)
                b_xb4 = Buf("xb4")
                mixb = sb("mixb", [128, 8, TB], BF16, ctx=ph)
                b_mixb = Buf("mixb", multi=True)
                m2 = sb("m2", [128, 8, TB], ctx=ph)
                b_m2 = Buf("m2", multi=True)
                h2s = sb("h2s", [128, 8, TB], BF16, ctx=ph)
                b_h2s = Buf("h2s", multi=True)
                u1 = [sb(f"u1{i}", [128, TB], ctx=ph) for i in range(2)]
                b_u1 = [Buf(f"u1{i}") for i in range(2)]
                u2 = [sb(f"u2{i}", [128, TB], ctx=ph) for i in range(2)]
                b_u2 = [Buf(f"u2{i}") for i in range(2)]
                for tb in range(NB):
                    g = 0 if tb < TS // TB else 1
                    tsl = slice(tb * TB, (tb + 1) * TB)
                    S.dma("sp", brb[:], brT.rearrange("(kc p) t -> p kc t", p=128)[:, :, tsl],
                          reads=[b_br], writes=[b_brb])
                    S.dma("sp", xb4[:], xT_v[:, :, tsl], reads=[b_xT], writes=[b_xb4])
                    for oc in range(8):
                        gi = oc % 2
                        S.dma("sp", gts[gi][:],
                              gT[:, tsl].rearrange("(a c p) t -> p a c t", a=3, p=128)[:, :, oc, :],
                              reads=[b_gT], writes=[b_gts[gi]])
                        osl = slice(oc * 128, (oc + 1) * 128)
                        bo = 3 * (oc % 2)
                        for bi, (wt, k0, nk_) in enumerate(((wpa, 0, 4), (wpb, 4, 8), (wpc, 12, 4))):
                            for kc in range(nk_):
                                S.op("pe", lambda e: e.matmul(
                                    psum[bo + bi][:], wt[:, kc, osl], brb[:, k0 + kc, :], start=(kc == 0), stop=(kc == nk_ - 1)),
                                    reads=[b_w4, b_brb], writes=[psb[bo + bi]], inc=(kc == nk_ - 1))
                        ui = oc % 2
                        S.op("dve", lambda e: e.tensor_tensor(u1[ui][:], psum[bo + 0][:], gts[gi][:, 0, :], ALU.mult),
                             reads=[psb[bo + 0], b_gts[gi]], writes=[b_u1[ui]])
                        S.op("dve", lambda e: e.tensor_tensor(u2[ui][:], psum[bo + 1][:], gts[gi][:, 1, :], ALU.mult),
                             reads=[psb[bo + 1], b_gts[gi]], writes=[b_u2[ui]])
                        S.op("pool", lambda e: e.tensor_tensor(u1[ui][:], u1[ui][:], u2[ui][:], ALU.add),
                             reads=[b_u1[ui], b_u2[ui]], writes=[b_u1[ui]])
                        S.op("dve", lambda e: e.tensor_tensor(u2[ui][:], psum[bo + 2][:], gts[gi][:, 2, :], ALU.mult),
                             reads=[psb[bo + 2], b_gts[gi]], writes=[b_u2[ui]])
                        S.op("pool", lambda e: e.tensor_tensor(mixb[:, oc, :], u1[ui][:], u2[ui][:], ALU.add),
                             reads=[b_u1[ui], b_u2[ui]], writes=[b_mixb])
                    for oc in range(8):
                        pb = (6 + oc) % 7
                        osl = slice(oc * 128, (oc + 1) * 128)
                        for kc in range(8):
                            S.op("pe", lambda e: e.matmul(
                                psum[pb][:], wo[:, kc, osl], mixb[:, kc, :], start=(kc == 0), stop=(kc == 7)),
                                reads=[b_w4, b_mixb], writes=[psb[pb]], inc=(kc == 7))
                        evac(oc, m2[:, oc, :], psum[pb][:], [psb[pb]], [b_m2])
                    rms_rstd(lambda kc: m2[:, kc, :], b_m2, 8, TB, 1.0 / D)
                    for kc in range(8):
                        ui = kc % 2
                        S.op("dve", lambda e: e.scalar_tensor_tensor(
                            u1[ui][:], m2[:, kc, :], modc[:, 2, kc, g:g + 1], rstd[:], ALU.mult, ALU.mult),
                            reads=[b_m2, b_modc, b_rstd], writes=[b_u1[ui]])
                        S.op("pool", lambda e: e.tensor_tensor(
                            xb4[:, kc, :], xb4[:, kc, :], u1[ui][:], ALU.add),
                            reads=[b_xb4, b_u1[ui]], writes=[b_xb4])
                    S.dma("pool", xT_v[:, :, tsl], xb4[:], reads=[b_xb4], writes=[b_xT])
                    norm_mod_block(xb4, b_xb4, tb, 1, lambda kc: h2s[:, kc, :], b_h2s)
                    S.dma("pool", hTd_v[:, :, tsl], h2s[:], reads=[b_h2s], writes=[b_hTd])
                S.barrier()
            if stop_after == "P4":
                break

            with ExitStack() as ph:
                hT = sb("hT2", [128, 8, NT], BF16, ctx=ph)
                b_hT2 = Buf("hT2", multi=True)
                for tb in range(NB):
                    S.dma("sp", hT[:, :, tb * TB:(tb + 1) * TB], hTd_v[:, :, tb * TB:(tb + 1) * TB], reads=[b_hTd], writes=[b_hT2])
                wf = [sb(f"f_wf{i}", [128, 8, 256], ctx=ph) for i in range(2)]
                b_wf = [Buf(f"f_wf{i}") for i in range(2)]
                wb = [sb(f"f_wb{i}", [128, 8, 256], BF16, ctx=ph) for i in range(2)]
                b_wb = [Buf(f"f_wb{i}") for i in range(2)]
                sg = [sb(f"f_sg{i}", [128, TB], ctx=ph) for i in range(2)]
                b_sg = [Buf(f"f_sg{i}") for i in range(2)]
                ao = [sb(f"f_ao{i}", [128, TB], BF16, ctx=ph) for i in range(3)]
                b_ao = [Buf(f"f_ao{i}") for i in range(3)]

                def load_wf(i, fc):
                    S.dma("sp", wf[i][:, :, 0:128], w_gate[l, :, fc * 128:(fc + 1) * 128].rearrange("(kc p) m -> p kc m", p=128),
                          writes=[b_wf[i]])
                    S.dma("sp", wf[i][:, :, 128:256], w_up[l, :, fc * 128:(fc + 1) * 128].rearrange("(kc p) m -> p kc m", p=128),
                          writes=[b_wf[i]])
                    S.op("pool", lambda e: e.tensor_copy(wb[i][:], wf[i][:]), reads=[b_wf[i]], writes=[b_wb[i]])
                it = 0
                load_wf(0, 0)
                for fc in range(22):
                    i = fc % 2
                    if fc + 1 < 22:
                        load_wf((fc + 1) % 2, fc + 1)
                    for tb in range(NB):
                        pg, pu = (it % 2) * 2, (it % 2) * 2 + 1
                        si, ai = it % 2, it % 3
                        it += 1
                        tsl = slice(tb * TB, (tb + 1) * TB)
                        for (pbk, c0) in ((pg, 0), (pu, 128)):
                            for kc in range(8):
                                S.op("pe", lambda e: e.matmul(
                                    psum[pbk][:], wb[i][:, kc, c0:c0 + 128], hT[:, kc, tsl], start=(kc == 0), stop=(kc == 7)),
                                    reads=[b_wb[i], b_hT2], writes=[psb[pbk]], inc=(kc == 7))
                        S.op("act", lambda e: e.activation(out=sg[si][:], in_=psum[pg][:], func=AF.Silu),
                             reads=[psb[pg]], writes=[b_sg[si]])
                        S.op("dve", lambda e: e.tensor_tensor(ao[ai][:], psum[pu][:], sg[si][:], ALU.mult),
                             reads=[psb[pu], b_sg[si]], writes=[b_ao[ai]])
                        S.dma("pool", actT[fc * 128:(fc + 1) * 128, tsl], ao[ai][:], reads=[b_ao[ai]], writes=[b_actT])
                S.barrier()

            with ExitStack() as ph:
                wd = sb("wd", [128, 22, D], BF16, ctx=ph)
                b_wd = Buf("wd", multi=True)
                wst = [sb(f"d_wst{i}", [128, 1, D], ctx=ph) for i in range(2)]
                b_wst = [Buf(f"d_wst{i}") for i in range(2)]
                for kc in range(22):
                    load_weight_bf16(wd[:, kc:kc + 1, :], b_wd,
                                     w_down[l, kc * 128:(kc + 1) * 128, :].rearrange("(kc p) m -> p kc m", p=128),
                                     wst, b_wst, kc, lambda t: t[:])
                ab = [sb(f"d_ab{i}", [128, 22, TB], BF16, ctx=ph) for i in range(2)]
                b_ab = [Buf(f"d_ab{i}") for i in range(2)]
                xb6 = [sb(f"xb6{i}", [128, 8, TB], ctx=ph) for i in range(2)]
                b_xb6 = [Buf(f"xb6{i}") for i in range(2)]
                f2 = sb("f2", [128, 8, TB], ctx=ph)
                b_f2 = Buf("f2", multi=True)
                u1 = [sb(f"d_u1{i}", [128, TB], ctx=ph) for i in range(2)]
                b_u1 = [Buf(f"d_u1{i}") for i in range(2)]
                for tb in range(NB):
                    i = tb % 2
                    g = 0 if tb < TS // TB else 1
                    tsl = slice(tb * TB, (tb + 1) * TB)
                    S.dma("sp", ab[i][:], actT.rearrange("(kc p) t -> p kc t", p=128)[:, :, tsl],
                          reads=[b_actT], writes=[b_ab[i]])
                    S.dma("sp", xb6[i][:], xT_v[:, :, tsl], reads=[b_xT], writes=[b_xb6[i]])
                    for oc in range(8):
                        pb = oc % 4
                        osl = slice(oc * 128, (oc + 1) * 128)
                        for kc in range(22):
                            S.op("pe", lambda e: e.matmul(
                                psum[pb][:], wd[:, kc, osl], ab[i][:, kc, :], start=(kc == 0), stop=(kc == 21)),
                                reads=[b_wd, b_ab[i]], writes=[psb[pb]], inc=(kc == 21))
                        evac(oc, f2[:, oc, :], psum[pb][:], [psb[pb]], [b_f2])
                    rms_rstd(lambda kc: f2[:, kc, :], b_f2, 8, TB, 1.0 / D)
                    for kc in range(8):
                        ui = kc % 2
                        S.op("dve", lambda e: e.scalar_tensor_tensor(
                            u1[ui][:], f2[:, kc, :], modc[:, 5, kc, g:g + 1], rstd[:], ALU.mult, ALU.mult),
                            reads=[b_f2, b_modc, b_rstd], writes=[b_u1[ui]])
                        S.op("pool", lambda e: e.tensor_tensor(
                            xb6[i][:, kc, :], xb6[i][:, kc, :], u1[ui][:], ALU.add),
                            reads=[b_xb6[i], b_u1[ui]], writes=[b_xb6[i]])
                    S.dma("pool", xT_v[:, :, tsl], xb6[i][:], reads=[b_xb6[i]], writes=[b_xT])
                S.barrier()
            if stop_after == "P6":
                break

        S.barrier()
        with ExitStack() as ph:
            xi = [sb(f"fx{i}", [128, 8, 128], ctx=ph) for i in range(2)]
            b_xi = [Buf(f"fx{i}") for i in range(2)]
            yo = [sb(f"fy{i}", [128, D], ctx=ph) for i in range(2)]
            b_yo = [Buf(f"fy{i}") for i in range(2)]
            for tt in range(NT // 128):
                i = tt % 2
                S.dma("sp", xi[i][:], xT_v[:, :, tt * 128:(tt + 1) * 128], reads=[b_xT], writes=[b_xi[i]])
                for half in range(2):
                    pb = (tt * 2 + half) % 8
                    for j in range(4):
                        kc = half * 4 + j
                        S.op("pe", lambda e, kc=kc, j=j, pb=pb, i=i: e.transpose(
                            psum[pb][:, j * 128:(j + 1) * 128], xi[i][:, kc, :], ident),
                            reads=[b_xi[i], b_cst], writes=[psb[pb]], inc=(j == 3))
                    evac(half, yo[i][:, half * 512:(half + 1) * 512], psum[pb][:], [psb[pb]], [b_yo[i]])
                dst = y_s[tt * 128:(tt + 1) * 128, :] if tt < TS // 128 else \
                    y_p[(tt - TS // 128) * 128:(tt - TS // 128 + 1) * 128, :]
                S.dma("sp", dst, yo[i][:], reads=[b_yo[i]], writes=[b_out])
        S.barrier()
        S.finish()
    print(f"[build] instructions={S.nins} waits={S.nwait}", flush=True)
    return nc


def _consts():
    c = np.zeros((128, NCONST, 128), np.float32)
    c[:, 0, :] = np.eye(128, dtype=np.float32)
    c[:, 1, :] = 1.0
    for i in range(64):
        c[2 * i + 1, 2, 2 * i] = -1.0
        c[2 * i, 2, 2 * i + 1] = 1.0
    p = np.arange(128)
    c[:, 3, :] = (p[:, None] <= p[None, :]).astype(np.float32)
    c[:, 4, :] = (p[:, None] >= p[None, :]).astype(np.float32)
    blk = lambda s_: (p[:, None] // s_ == p[None, :] // s_)
    c[:, 5, :] = blk(16).astype(np.float32)
    c[:, 6, :] = (blk(32) & ~blk(16)).astype(np.float32)
    c[:, 7, :] = (blk(64) & ~blk(32)).astype(np.float32)
    c[:, 8, :] = (~blk(64)).astype(np.float32)
    return c


def _rope_tables():
    rows = TS // 64
    row_id = np.repeat(np.arange(rows, dtype=np.float32), 64)
    col_id = np.tile(np.arange(64, dtype=np.float32), rows)
    n_freq = 32
    inv_freq = (np.float32(10000.0) ** (-np.arange(n_freq, dtype=np.float32) / np.float32(n_freq))).astype(np.float32)
    ang = np.concatenate([row_id[:, None] * inv_freq, col_id[:, None] * inv_freq], axis=-1).astype(np.float32)
    cos = np.cos(ang).astype(np.float32)
    sin = np.sin(ang).astype(np.float32)
    t = np.zeros((128, 2, TS), np.float32)
    t[:, 0, :] = np.repeat(cos.T, 2, axis=0)
    t[:, 1, :] = np.repeat(sin.T, 2, axis=0)
    return t


def _pack_pp(inp):
    pp = np.zeros((DEPTH, 128, PPW), np.float32)
    for l in range(DEPTH):
        pp[l, :, PP_BMOD:PP_BMOD + 48] = inp["b_mod"][l].reshape(48, 128).T
        for off, nm in ((PP_GPRE1, "g_pre1"), (PP_GPOST1, "g_post1"), (PP_GPRE2, "g_pre2"), (PP_GPOST2, "g_post2")):
            pp[l, :, off:off + 8] = inp[nm][l].reshape(8, 128).T
        pp[l, :, PP_CONVW:PP_CONVW + 12] = inp["conv_w"][l].reshape(3, 4, 128).transpose(2, 1, 0).reshape(128, 12)
        pp[l, :, PP_DNCONVW:PP_DNCONVW + 36] = inp["dn_conv_w"][l].reshape(3, 12, 128).transpose(2, 1, 0).reshape(128, 36)
        pp[l, :, PP_GQN] = inp["g_qn"][l]
        pp[l, :, PP_GKN] = inp["g_kn"][l]
        pp[l, :, PP_DNG] = inp["dn_norm_g"][l]
        pp[l, :, PP_ALOG:PP_ALOG + 8] = inp["dn_a_log"][l].reshape(1, 8)
        pp[l, :, PP_DTB:PP_DTB + 8] = inp["dn_dt_bias"][l].reshape(1, 8)
    return pp


WEIGHTS = ("w_mod", "w_in", "w_pa", "w_pb", "w_pc", "w_o", "w_gate", "w_up", "w_down")


def make_in_maps(inp, cores):
    pp = _pack_pp(inp)
    consts = _consts()
    rope = _rope_tables()
    maps = []
    for c in cores:
        cvec = np.zeros((128, 8, 2), np.float32)
        cvec[:, :, 0] = inp["c"][c].reshape(8, 128).T
        cvec[:, :, 1] = inp["c_ctx"].reshape(8, 128).T
        m = {
            "xs": np.ascontiguousarray(inp["x_sample"][c]),
            "xp": np.ascontiguousarray(inp["x_prompt"][4 * c:4 * c + 4].reshape(TP, D)),
            "cvec": cvec, "pp": pp, "consts": consts, "rope": rope,
            "ck": np.ascontiguousarray(inp["cache_k"][c].reshape(DEPTH, PAST, 256)),
            "cvv": np.ascontiguousarray(inp["cache_v"][c].reshape(DEPTH, PAST, 256)),
            "sdn": np.ascontiguousarray(inp["state_dn"][c]),
        }
        for w in WEIGHTS:
            m[w] = inp[w]
        maps.append(m)
    return maps


def kernel(**inputs):
    inp = {k: np.asarray(v) for k, v in inputs.items()}
    nc = build_program()
    maps = make_in_maps(inp, range(8))
    res = run_bass_kernel_spmd(nc, maps, core_ids=list(range(8)))
    r = res.results
    y_s = np.stack([r[c]["y_s"] for c in range(8)], 0)
    y_p = np.concatenate([r[c]["y_p"].reshape(4, 256, D) for c in range(8)], 0)
    nk = np.concatenate([r[c]["nk"].reshape(4, DEPTH, 256, 2, 128) for c in range(8)], 0)
    nv = np.concatenate([r[c]["nv"].reshape(4, DEPTH, 256, 2, 128) for c in range(8)], 0)
    ns = np.concatenate([r[c]["ns"] for c in range(8)], 0)
    return (y_p, y_s, nk, nv, ns)
```
